# Optimizing a Trainium2 kernel written in Bass

```python
import jax, jax.numpy as jnp
from jax import lax
import numpy as np

D_MODEL = 1024
BATCH = 8
SEQ = 4096
DEPTH = 2

CTX_LEN = 256
GRID_W = 64
N_BRANCH = 3
GLA_WIDTH = D_MODEL
GLA_HEADS = 4
GLA_HV = GLA_WIDTH // GLA_HEADS
GLA_HK = GLA_HV // 2
GLA_RANK = 16
GLA_TAU = 16.0
HGRN_WIDTH = D_MODEL
HGRN_HEADS = D_MODEL // 128
HGRN_HV = HGRN_WIDTH // HGRN_HEADS
HGRN_EXPAND = 128
HGRN_F = HGRN_HEADS * HGRN_EXPAND
ATT_HD = 128
ATT_HQ = D_MODEL // ATT_HD
ATT_HKV = ATT_HQ // 4
ATT_WIDTH = ATT_HQ * ATT_HD
WINDOW = 128
ATT_BLOCK = 128
ROPE_BASE = 10000.0
CHUNK = 16
EPS = 1e-6
IN_SIZES = (GLA_HEADS * GLA_HK, GLA_HEADS * GLA_HK, GLA_WIDTH, GLA_WIDTH, 2 * GLA_RANK,
            HGRN_F, 2 * HGRN_F, HGRN_WIDTH, HGRN_WIDTH,
            ATT_WIDTH, ATT_HKV * ATT_HD, ATT_HKV * ATT_HD, ATT_WIDTH,
            N_BRANCH * D_MODEL)
N_IN = sum(IN_SIZES)

kernel_name = 'hybrid_gla_hgrn2_swa_prefix_dit_block'


def _rms_norm(x, g):
    xf = x.astype(jnp.float32)
    y = xf * lax.rsqrt(jnp.mean(xf * xf, axis=-1, keepdims=True) + EPS)
    return (y * g.astype(jnp.float32)).astype(x.dtype)


def _modulation(cond, w_ada, b_ada):
    m = jnp.matmul(jax.nn.silu(cond), w_ada) + b_ada
    return tuple(t[:, None, :] for t in jnp.split(m, 3, axis=-1))


def _split_in(p):
    return jnp.split(p, np.cumsum(IN_SIZES)[:-1].tolist(), axis=-1)


def _to_heads(t, n_heads):
    bsz, t_len, w = t.shape
    return t.reshape(bsz, t_len, n_heads, w // n_heads).transpose(0, 2, 1, 3)


def _flip(t):
    return jnp.flip(t, axis=2)


def _chunked_recurrence(q, k, v, log_a, s0):
    f32 = jnp.float32
    bsz, nh, t_len, _ = q.shape
    dv = v.shape[-1]
    n = t_len // CHUNK

    def blk(t):
        return t.astype(f32).reshape(bsz, nh, n, CHUNK, t.shape[-1])

    q, k, v, log_a = blk(q), blk(k), blk(v), blk(log_a)
    b = jnp.cumsum(log_a, axis=3)
    b_end = b[:, :, :, -1:, :]
    q_dec = q * jnp.exp(b)
    att = jnp.einsum('bhnik,bhnjk->bhnij', q_dec, k * jnp.exp(-b))
    att = jnp.where(jnp.tril(jnp.ones((CHUNK, CHUNK), dtype=bool)), att, 0.0)
    o_intra = jnp.einsum('bhnij,bhnjv->bhniv', att, v)
    k_end = k * jnp.exp(b_end - b)
    decay = jnp.exp(b_end[:, :, :, 0, :])

    def step(state, inp):
        qd, ke, vc, dc = inp
        o = jnp.einsum('bhik,bhkv->bhiv', qd, state)
        state = dc[..., None] * state + jnp.einsum('bhjk,bhjv->bhkv', ke, vc)
        return state, o

    xs = tuple(jnp.moveaxis(t, 2, 0) for t in (q_dec, k_end, v, decay))
    s_final, o_inter = lax.scan(step, s0, xs)
    o = o_intra + jnp.moveaxis(o_inter, 0, 2)
    return o.reshape(bsz, nh, t_len, dv), s_final


def _bidir_recurrence(lat, ctx):
    qc, kcf, kcb, vc, lcf, lcb = ctx
    q, kf, kb, v, lgf, lgb = lat
    s0 = jnp.zeros(qc.shape[:2] + (qc.shape[-1], vc.shape[-1]), jnp.float32)
    oc_f, s_f = _chunked_recurrence(qc, kcf, vc, lcf, s0)
    oc_b, s_b = _chunked_recurrence(_flip(qc), _flip(kcb), _flip(vc), _flip(lcb), s0)
    o_f, _ = _chunked_recurrence(q, kf, v, lgf, s_f)
    o_b, _ = _chunked_recurrence(_flip(q), _flip(kb), _flip(v), _flip(lgb), s_b)
    return o_f + _flip(o_b), oc_f + _flip(oc_b)


def _gla_streams(q, k, v, lr, w_a2, b_a2):
    q = _to_heads(q, GLA_HEADS) * (GLA_HK ** -0.5)
    k = _to_heads(k, GLA_HEADS)
    v = _to_heads(v, GLA_HEADS)
    bsz, t_len, _ = lr.shape
    lr = lr.reshape(bsz, t_len, 2, GLA_RANK)
    z = jnp.einsum('btdr,drk->dbtk', lr, w_a2) + b_a2[:, None, None, :]
    log_a = jax.nn.log_sigmoid(z.astype(jnp.float32)) / GLA_TAU
    return (q, k, k, v, _to_heads(log_a[0], GLA_HEADS), _to_heads(log_a[1], GLA_HEADS))


def _hgrn_streams(q, f, i, lb):
    bsz, t_len, _ = f.shape
    z = f.reshape(bsz, t_len, 2, HGRN_F).astype(jnp.float32)
    log_f = jnp.logaddexp(jnp.log(lb), jnp.log1p(-lb) + jax.nn.log_sigmoid(z))
    k = -jnp.expm1(log_f)
    h = HGRN_HEADS
    return (_to_heads(q, h), _to_heads(k[:, :, 0], h), _to_heads(k[:, :, 1], h), _to_heads(i, h),
            _to_heads(log_f[:, :, 0], h), _to_heads(log_f[:, :, 1], h))


def _head_norm_gate(o, gain, gate):
    bsz, nh, t_len, dv = o.shape
    o = _rms_norm(jnp.transpose(o, (0, 2, 1, 3)), gain).reshape(bsz, t_len, nh * dv)
    return o.astype(gate.dtype) * jax.nn.silu(gate)


def _rope_1d(u, pos):
    r = u.shape[-1] // 2
    inv = ROPE_BASE ** (-jnp.arange(r, dtype=jnp.float32) / r)
    ang = pos.astype(jnp.float32)[:, None] * inv
    cos, sin = jnp.cos(ang)[:, None, :], jnp.sin(ang)[:, None, :]
    uf = u.astype(jnp.float32)
    u1, u2 = uf[..., :r], uf[..., r:]
    return jnp.concatenate([u1 * cos - u2 * sin, u2 * cos + u1 * sin], axis=-1).astype(u.dtype)


def _axial_rope(t, row, col):
    half = t.shape[-1] // 2
    return jnp.concatenate([_rope_1d(t[..., :half], row), _rope_1d(t[..., half:], col)], axis=-1)


def _window_attention(q, k, v, kc, vc, sink):
    f32 = jnp.float32
    bsz, s_len, _, hd = q.shape
    nb = s_len // ATT_BLOCK
    grp = ATT_HQ // ATT_HKV
    qb = (q * hd ** -0.5).reshape(bsz, nb, ATT_BLOCK, ATT_HKV, grp, hd)

    def band(t):
        tp = jnp.pad(t, ((0, 0), (ATT_BLOCK, ATT_BLOCK), (0, 0), (0, 0)))
        tp = tp.reshape(bsz, nb + 2, ATT_BLOCK, ATT_HKV, hd)
        return jnp.concatenate([tp[:, :-2], tp[:, 1:-1], tp[:, 2:]], axis=2)

    kw, vw = band(k), band(v)
    qi = jnp.arange(ATT_BLOCK)
    kj = jnp.arange(3 * ATT_BLOCK)
    blk = jnp.arange(nb)
    rel = kj[None, :] - ATT_BLOCK - qi[:, None]
    kpos = blk[:, None] * ATT_BLOCK - ATT_BLOCK + kj[None, :]
    mask = (jnp.abs(rel) <= WINDOW)[None] & ((kpos >= 0) & (kpos < s_len))[:, None, :]
    s_loc = jnp.einsum('bnqhgd,bnkhd->bhgnqk', qb, kw).astype(f32)
    s_loc = jnp.where(mask, s_loc, -jnp.inf)
    s_ctx = jnp.einsum('bnqhgd,bchd->bhgnqc', qb, kc).astype(f32)
    sk = jnp.broadcast_to(sink.astype(f32).reshape(1, ATT_HKV, grp, 1, 1, 1), s_ctx.shape[:-1] + (1,))
    p = jax.nn.softmax(jnp.concatenate([sk, s_ctx, s_loc], axis=-1), axis=-1).astype(v.dtype)
    n_ctx = kc.shape[1]
    out = (jnp.einsum('bhgnqc,bchd->bnqhgd', p[..., 1:1 + n_ctx], vc)
           + jnp.einsum('bhgnqk,bnkhd->bnqhgd', p[..., 1 + n_ctx:], vw))
    return out.reshape(bsz, s_len, ATT_HQ * hd)


def _context_attention(qc, kc, vc, sink):
    f32 = jnp.float32
    bsz, n_ctx, _, hd = qc.shape
    grp = ATT_HQ // ATT_HKV
    qs = (qc * hd ** -0.5).reshape(bsz, n_ctx, ATT_HKV, grp, hd)
    s = jnp.einsum('bqhgd,bkhd->bhgqk', qs, kc).astype(f32)
    sk = jnp.broadcast_to(sink.astype(f32).reshape(1, ATT_HKV, grp, 1, 1), s.shape[:-1] + (1,))
    p = jax.nn.softmax(jnp.concatenate([sk, s], axis=-1), axis=-1).astype(vc.dtype)
    out = jnp.einsum('bhgqk,bkhd->bqhgd', p[..., 1:], vc)
    return out.reshape(bsz, n_ctx, ATT_HQ * hd)


def _merge(ys, mg, w_branch, w_out):
    y = jnp.stack(ys, axis=2)
    proj = jnp.einsum('btnw,nwd->btnd', y, w_branch)
    gates = jax.nn.sigmoid(mg.reshape(mg.shape[:-1] + (N_BRANCH, D_MODEL)).astype(jnp.float32))
    merged = jnp.sum(gates.astype(proj.dtype) * proj, axis=2)
    return jnp.matmul(merged, w_out)


def _layer(x, ctx, c, c_ctx, row, col, norm_g, w_ada, b_ada, w_in, gla_w_a2, gla_b_a2, gla_norm_g,
           hgrn_lb, hgrn_norm_g, attn_sink, w_branch, w_out, update_ctx):
    shift, scale, gate = _modulation(c, w_ada, b_ada)
    shift_c, scale_c, gate_c = _modulation(c_ctx[None], w_ada, b_ada)
    h = _rms_norm(x, norm_g) * (1.0 + scale) + shift
    hc = _rms_norm(ctx, norm_g) * (1.0 + scale_c) + shift_c
    (ga_q, ga_k, ga_v, ga_g, ga_lr, hg_q, hg_f, hg_i, hg_g,
     wa_q, wa_k, wa_v, wa_g, mg) = _split_in(jnp.matmul(h, w_in))
    (gac_q, gac_k, gac_v, gac_g, gac_lr, hgc_q, hgc_f, hgc_i, hgc_g,
     wac_q, wac_k, wac_v, wac_g, mgc) = _split_in(jnp.matmul(hc, w_in))

    o_gla, oc_gla = _bidir_recurrence(_gla_streams(ga_q, ga_k, ga_v, ga_lr, gla_w_a2, gla_b_a2),
                                      _gla_streams(gac_q, gac_k, gac_v, gac_lr, gla_w_a2, gla_b_a2))
    y_gla = _head_norm_gate(o_gla, gla_norm_g, ga_g)

    o_hg, oc_hg = _bidir_recurrence(_hgrn_streams(hg_q, hg_f, hg_i, hgrn_lb),
                                    _hgrn_streams(hgc_q, hgc_f, hgc_i, hgrn_lb))
    y_hg = _head_norm_gate(o_hg, hgrn_norm_g, hg_g)

    bsz, s_len, _ = x.shape
    n_ctx = ctx.shape[1]
    q = _axial_rope(wa_q.reshape(bsz, s_len, ATT_HQ, ATT_HD), row, col)
    k = _axial_rope(wa_k.reshape(bsz, s_len, ATT_HKV, ATT_HD), row, col)
    v = wa_v.reshape(bsz, s_len, ATT_HKV, ATT_HD)
    kc = wac_k.reshape(bsz, n_ctx, ATT_HKV, ATT_HD)
    vc = wac_v.reshape(bsz, n_ctx, ATT_HKV, ATT_HD)
    y_att = _window_attention(q, k, v, kc, vc, attn_sink) * jax.nn.silu(wa_g)

    x_new = x + gate * _merge((y_gla, y_hg, y_att), mg, w_branch, w_out)
    if update_ctx:
        yc_gla = _head_norm_gate(oc_gla, gla_norm_g, gac_g)
        yc_hg = _head_norm_gate(oc_hg, hgrn_norm_g, hgc_g)
        qc = wac_q.reshape(bsz, n_ctx, ATT_HQ, ATT_HD)
        yc_att = _context_attention(qc, kc, vc, attn_sink) * jax.nn.silu(wac_g)
        ctx = ctx + gate_c * _merge((yc_gla, yc_hg, yc_att), mgc, w_branch, w_out)
    return x_new, ctx


def setup_inputs(seed: int = 0) -> dict:
    key = jax.random.key(seed)
    ks = jax.random.split(key, 20)
    f32 = jnp.float32

    def nrm(k, shape, scale):
        return jax.random.normal(k, shape, f32) * scale

    d = D_MODEL
    return {
        'x': nrm(ks[0], (BATCH, SEQ, d), 1.0),
        'c': nrm(ks[1], (BATCH, d), 1.0),
        'ctx': nrm(ks[2], (BATCH, CTX_LEN, d), 1.0),
        'c_ctx': nrm(ks[3], (d,), 1.0),
        'norm_g': 1.0 + nrm(ks[4], (DEPTH, d), 0.1),
        'w_ada': nrm(ks[5], (DEPTH, d, 3 * d), 0.5 * d ** -0.5),
        'b_ada': nrm(ks[6], (DEPTH, 3 * d), 0.02),
        'w_in': nrm(ks[7], (DEPTH, d, N_IN), d ** -0.5),
        'gla_w_a2': nrm(ks[8], (DEPTH, 2, GLA_RANK, GLA_HEADS * GLA_HK), GLA_RANK ** -0.5),
        'gla_b_a2': nrm(ks[9], (DEPTH, 2, GLA_HEADS * GLA_HK), 0.1),
        'gla_norm_g': 1.0 + nrm(ks[10], (DEPTH, GLA_HV), 0.1),
        'hgrn_lb_logits': nrm(ks[11], (DEPTH, 2, HGRN_F), 1.0),
        'hgrn_norm_g': 1.0 + nrm(ks[12], (DEPTH, HGRN_HV), 0.1),
        'attn_sink': nrm(ks[13], (DEPTH, ATT_HQ), 1.0),
        'w_branch': nrm(ks[14], (DEPTH, N_BRANCH, D_MODEL, d), D_MODEL ** -0.5),
        'w_out': nrm(ks[15], (DEPTH, d, d), d ** -0.5),
        'final_g': 1.0 + nrm(ks[16], (d,), 0.1),
    }


def reference(x, c, ctx, c_ctx, norm_g, w_ada, b_ada, w_in, gla_w_a2, gla_b_a2, gla_norm_g,
              hgrn_lb_logits, hgrn_norm_g, attn_sink, w_branch, w_out, final_g):
    s_len = x.shape[1]
    n_rows = s_len // GRID_W
    row = jnp.repeat(jnp.arange(n_rows), GRID_W)
    col = jnp.tile(jnp.arange(GRID_W), n_rows)
    lb_cum = jnp.cumsum(jax.nn.softmax(hgrn_lb_logits.astype(jnp.float32), axis=0), axis=0)
    lower_bounds = lb_cum - lb_cum[0]
    for l in range(DEPTH):
        x, ctx = _layer(x, ctx, c, c_ctx, row, col, norm_g[l], w_ada[l], b_ada[l], w_in[l],
                        gla_w_a2[l], gla_b_a2[l], gla_norm_g[l], lower_bounds[l], hgrn_norm_g[l],
                        attn_sink[l], w_branch[l], w_out[l], l < DEPTH - 1)
    return _rms_norm(x, final_g)
```

```python
import numpy as np
from contextlib import ExitStack
import ml_dtypes
import concourse.bass as bass
import concourse.mybir as mybir
from concourse.bass_utils import run_bass_kernel_spmd

F32 = mybir.dt.float32
BF16 = mybir.dt.bfloat16
AF = mybir.ActivationFunctionType
ALU = mybir.AluOpType
AX = mybir.AxisListType

NDMASEM = 12
ALAG = 5
NT = 4352
NTL = 34
DM = 1024
NIN = 13856
EPS = 1e-6


class Sched:
    COMPUTE = ("pe", "act", "dve", "pool")
    QUEUES = ("sp", "act", "pool")

    def __init__(self, nc, stack):
        self.nc = nc
        self.ops = {e: [] for e in ("pe", "act", "dve", "pool", "sp")}
        self.sem = {}
        self.cnt = {}
        for e in self.COMPUTE:
            self.sem[e] = stack.enter_context(nc.semaphore("s_" + e))
            self.cnt[e] = 0
        self.dsem = {}
        self.dval = {}
        self.dnext = {}
        for q in self.QUEUES:
            self.dsem[q] = [stack.enter_context(nc.semaphore("d_%s%d" % (q, i))) for i in range(NDMASEM)]
            self.dval[q] = [0] * NDMASEM
            self.dnext[q] = 0
        self.waited = {e: {} for e in self.ops}
        self.last_w = {}
        self.readers = {}

    def _deps(self, eng, reads, writes):
        deps = []
        for r in reads:
            t = self.last_w.get(r)
            if t is not None:
                deps.append(t)
        for w in writes:
            t = self.last_w.get(w)
            if t is not None:
                deps.append(t)
            for t in self.readers.get(w, {}).values():
                deps.append(t)
        waits = []
        wd = self.waited[eng]
        best = {}
        for (sem, val, src) in deps:
            if src == "pe" and eng == "pe":
                continue
            k = id(sem)
            if wd.get(k, 0) >= val:
                continue
            if k not in best or best[k][1] < val:
                best[k] = (sem, val)
        for k, (sem, val) in best.items():
            wd[k] = val
            waits.append((sem, val))
        return waits

    def _commit(self, eng, reads, writes, token):
        for r in reads:
            self.readers.setdefault(r, {})[(eng, id(token[0]))] = token
        for w in writes:
            self.last_w[w] = token
            self.readers[w] = {}

    def op(self, eng, fn, reads=(), writes=()):
        waits = self._deps(eng, reads, writes)
        self.cnt[eng] += 1
        token = (self.sem[eng], self.cnt[eng], eng)
        self.ops[eng].append((fn, waits, (self.sem[eng], 1)))
        self._commit(eng, reads, writes, token)

    def dma(self, q, out, in_, reads=(), writes=(), **kw):
        i = self.dnext[q]
        self.dnext[q] = (i + 1) % NDMASEM
        sem = self.dsem[q][i]
        waits = self._deps(q, reads, writes)
        prev = self.dval[q][i]
        if prev > 0 and self.waited[q].get(id(sem), 0) < prev:
            self.waited[q][id(sem)] = prev
            waits.append((sem, prev))
        self.dval[q][i] = prev + 16
        token = (sem, prev + 16, "dma_" + q)

        def fn(e, out=out, in_=in_, kw=kw):
            return e.dma_start(out=out, in_=in_, **kw)

        self.ops[q].append((fn, waits, (sem, 16)))
        self._commit("dma_" + q, reads, writes, token)

    def barrier(self):
        toks = []
        for e in self.COMPUTE:
            if self.cnt[e] > 0:
                toks.append((self.sem[e], self.cnt[e]))
        for q in self.QUEUES:
            for i in range(NDMASEM):
                if self.dval[q][i] > 0:
                    toks.append((self.dsem[q][i], self.dval[q][i]))
        for eng in self.ops:
            waits = []
            for (sem, val) in toks:
                if self.waited[eng].get(id(sem), 0) < val:
                    self.waited[eng][id(sem)] = val
                    waits.append((sem, val))
            if waits:
                self.ops[eng].append((None, waits, None))
        self.last_w = {}
        self.readers = {}

    def emit(self):
        nc = self.nc
        with nc.Block() as block:
            def run(name):
                def body(e):
                    for (fn, waits, inc) in self.ops[name]:
                        for (sem, val) in waits:
                            e.wait_ge(sem, val)
                        if fn is not None:
                            ins = fn(e)
                            ins.then_inc(inc[0], inc[1])
                return body
            block.sync(run("sp"))
            block.tensor(run("pe"))
            block.scalar(run("act"))
            block.vector(run("dve"))
            block.gpsimd(run("pool"))


C_GQ, C_GK, C_GV, C_GG, C_LR = 0, 512, 1024, 2048, 3072
C_HQ, C_HF, C_HI, C_HG = 3104, 4128, 6176, 7200
C_AQ, C_AK, C_AV, C_AG, C_MG = 8224, 9248, 9504, 9760, 10784


class StopBuild(Exception):
    pass


class Builder:
    def __init__(self, nlayers=2, dbg=(), stop=None):
        self.nlayers = nlayers
        self.dbg = set(dbg)
        self.stop = stop
        self.nc = bass.Bass("TRN2", target_bir_lowering=False)
        self.rr = 0

    def din(self, name, shape, dt=F32):
        return self.nc.dram_tensor(name, list(shape), dt, kind="ExternalInput").ap()

    def scr(self, name, shape, dt):
        kind = "ExternalOutput" if name in self.dbg else "Internal"
        return self.nc.dram_tensor(name, list(shape), dt, kind=kind).ap()

    def I(self, eng, name, r=(), w=(), **kw):
        self.S.op(eng, lambda e: getattr(e, name)(**kw), reads=r, writes=w)

    def chk(self, ph, l):
        if self.stop == (ph, l):
            raise StopBuild()

    def TT(self, ph):
        self.uid = getattr(self, "uid", 0) + 1
        u = self.uid
        nc = self.nc
        T = lambda n, s, d: ph.enter_context(nc.sbuf_tensor("%s_u%d" % (n, u), s, d))
        P = lambda n, s, d: ph.enter_context(nc.psum_tensor("%s_u%d" % (n, u), s, d))
        return T, P

    def alt(self):
        self.rr ^= 1
        return "dve" if self.rr else "act"

    def copy(self, eng, out, in_, r, w):
        if eng == "act":
            self.I("act", "activation", r=r, w=w, out=out, in_=in_, func=AF.Copy)
        else:
            self.I(eng, "tensor_copy", r=r, w=w, out=out, in_=in_)

    def build(self):
        nc = self.nc
        L = self.nlayers
        self.x_in = self.din("x", [4096, DM])
        self.ctx_in = self.din("ctx", [256, DM])
        self.cc = self.din("cc", [128, 8, 2])
        self.ng_fm = self.din("ng_fm", [128, 2, 8])
        self.w_ada = self.din("w_ada", [2, DM, 3 * DM])
        self.bada_fm = self.din("bada_fm", [128, 2, 24])
        self.b_ada = self.din("b_ada", [2, 3 * DM])
        self.w_in = self.din("w_in", [2, DM, NIN])
        self.wa2p = self.din("wa2p", [2, 2, 33, 512])
        self.gla_ng = self.din("gla_ng_fm", [128, 2, 8])
        self.hg_ng = self.din("hg_ng_fm", [128, 2, 8])
        self.sink = self.din("sink", [2, 8])
        self.final_g = self.din("final_g", [DM])
        self.lb_fm = self.din("lb_fm", [128, 2, 2, 8])
        self.w_branch = self.din("w_branch", [2, 3, DM, DM])
        self.w_out = self.din("w_out", [2, DM, DM])
        self.c_ident = self.din("c_ident", [128, 128])
        self.c_mle = self.din("c_mle", [128, 128])
        self.c_mge = self.din("c_mge", [128, 128])
        self.c_rope = self.din("c_rope", [4096, 128])
        self.c_mle_u = self.din("c_mle_u", [128, 128], mybir.dt.uint32)
        self.c_mge_u = self.din("c_mge_u", [128, 128], mybir.dt.uint32)
        self.out = nc.dram_tensor("out", [4096, DM], F32, kind="ExternalOutput").ap()

        sc = self.scr
        self.xs = sc("xs", [4096, DM], F32)
        self.cs = sc("cs", [256, DM], F32)
        self.GQ = sc("GQ", [512, NT], BF16)
        self.GK = sc("GK", [512, NT], BF16)
        self.GLR = sc("GLR", [32, NT], F32)
        self.GV = sc("GV", [NT, DM], BF16)
        self.GG = sc("GG", [DM, NT], BF16)
        self.HQ = sc("HQ", [1024, NT], BF16)
        self.HZ = sc("HZ", [2048, NT], F32)
        self.HV = sc("HV", [NT, DM], BF16)
        self.HG = sc("HG", [DM, NT], BF16)
        self.AQ = sc("AQ", [NT, DM], BF16)
        self.AK = sc("AK", [NT, 256], BF16)
        self.AV = sc("AV", [NT, 256], BF16)
        self.AG = sc("AG", [NT, DM], BF16)
        self.MG = sc("MG", [3072, NT], BF16)
        self.OFG = sc("OFG", [DM, NT], F32)
        self.OFH = sc("OFH", [DM, NT], F32)
        self.YG = sc("YG", [DM, NT], BF16)
        self.YH = sc("YH", [DM, NT], BF16)
        self.YA = sc("YA", [DM, NT], BF16)

        with ExitStack() as st:
            self.S = Sched(nc, st)
            S = self.S
            T = lambda n, s, d: st.enter_context(nc.sbuf_tensor(n, s, d))
            self.identf = T("identf", [128, 128], F32)
            self.identb = T("identb", [128, 128], BF16)
            self.mle = T("mle", [128, 128], F32)
            self.mge = T("mge", [128, 128], F32)
            self.mle_u = T("mle_u", [128, 128], mybir.dt.uint32)
            self.mge_u = T("mge_u", [128, 128], mybir.dt.uint32)
            S.dma("sp", self.mle_u[:], self.c_mle_u, writes=["mle_u"])
            S.dma("sp", self.mge_u[:], self.c_mge_u, writes=["mge_u"])
            self.mleb = T("mleb", [128, 128], BF16)
            self.mgeb = T("mgeb", [128, 128], BF16)
            self.A_fm = T("A_fm", [128, 2, 2, 8], F32)
            self.B_fm = T("B_fm", [128, 2, 2, 8], F32)
            self.gate_bc = [[T("gate_bc%d%d" % (l, j), [128, DM], F32) for j in range(2)] for l in range(2)]
            self.eps_t = T("eps_t", [128, 1], F32)
            self.one_t = T("one_t", [128, 1], F32)
            S.dma("sp", self.identf[:], self.c_ident, writes=["identf"])
            S.dma("sp", self.mle[:], self.c_mle, writes=["mle"])
            S.dma("sp", self.mge[:], self.c_mge, writes=["mge"])
            self.I("dve", "tensor_copy", r=["identf"], w=["identb"], out=self.identb[:], in_=self.identf[:])
            self.I("dve", "tensor_copy", r=["mle"], w=["mleb"], out=self.mleb[:], in_=self.mle[:])
            self.I("dve", "tensor_copy", r=["mge"], w=["mgeb"], out=self.mgeb[:], in_=self.mge[:])
            self.I("pool", "memset", w=["eps_t"], ap=self.eps_t[:], constant=EPS)
            self.I("pool", "memset", w=["one_t"], ap=self.one_t[:], constant=1.0)

            self.phase_mod()
            try:
                for l in range(L):
                    with nc.sbuf_tensor("hT_%d" % l, [128, 8, NT], BF16) as hT:
                        self.hT = hT
                        self.phase_B(l)
                    self.chk("B", l)
                    self.phase_rec(l, "gla")
                    self.chk("G", l)
                    self.phase_rec(l, "hgrn")
                    self.chk("H", l)
                    with nc.sbuf_tensor("wbr_%d" % l, [128, 3, 8, DM], BF16) as wbr, \
                            nc.sbuf_tensor("wo_%d" % l, [128, 8, DM], BF16) as wo:
                        self.wbr, self.wo = wbr, wo
                        for n in range(3):
                            for half in range(2):
                                S.dma("pool", wbr[:, n, half * 4:(half + 1) * 4, :],
                                      self.w_branch[l, n, half * 512:(half + 1) * 512, :].rearrange("(k p) d -> p k d", p=128))
                        for half in range(2):
                            S.dma("pool", wo[:, half * 4:(half + 1) * 4, :],
                                  self.w_out[l, half * 512:(half + 1) * 512, :].rearrange("(k p) d -> p k d", p=128))
                        self.phase_att(l)
                        self.phase_merge(l, last=(l == L - 1))
                    self.chk("T", l)
                    self.chk("M", l)
            except StopBuild:
                pass
            S.barrier()
            S.emit()
        return nc

    def phase_mod(self):
        nc, S = self.nc, self.S
        with ExitStack() as ph:
            T, P = self.TT(ph)
            cct = T("cct", [128, 8, 2], F32)
            sct = T("sct", [128, 8, 2], F32)
            scb = T("scb", [128, 16, 128], F32)
            wada = T("wada", [128, 8, 3 * DM], F32)
            ngt = T("ngt", [128, 2, 8], F32)
            bfm = T("bfm", [128, 2, 24], F32)
            bbc = T("bbc", [128, DM], F32)
            mod = T("mod", [128, 16, 2], F32)
            tmpm = T("tmpm", [128, 8, 2], F32)
            ps_m = P("ps_m", [128, 16, 2], F32)
            ps_g = P("ps_g", [128, DM], F32)
            S.dma("sp", cct[:], self.cc, writes=["cct"])
            S.dma("sp", ngt[:], self.ng_fm, writes=["ngt"])
            S.dma("sp", bfm[:], self.bada_fm, writes=["bfm"])
            self.I("act", "activation", r=["cct"], w=["sct"], out=sct[:], in_=cct[:], func=AF.Silu)
            self.I("dve", "tensor_copy", r=["sct"], w=["scb"], out=scb[:],
                   in_=sct[:].rearrange("p k j -> p (k j)").unsqueeze(2).to_broadcast([128, 16, 128]))
            for l in range(self.nlayers):
                for kc in range(8):
                    S.dma("sp", wada[:, kc, :], self.w_ada[l, kc * 128:(kc + 1) * 128, :], writes=["wada%d" % kc])
                S.dma("sp", bbc[:], self.b_ada[l, 2 * DM:3 * DM].partition_broadcast(128), writes=["bbc"])
                wk = ["wada%d" % kc for kc in range(8)]
                for f in range(16):
                    for kc in range(8):
                        self.I("pe", "matmul", r=[wk[kc], "sct"], w=["ps_m"], out=ps_m[:, f, :],
                               lhsT=wada[:, kc, f * 128:(f + 1) * 128], rhs=sct[:, kc, :], start=(kc == 0), stop=(kc == 7))
                self.I("dve", "tensor_tensor", r=["ps_m", "bfm"], w=["mod"], out=mod[:], in0=ps_m[:],
                       in1=bfm[:, l, 0:16].unsqueeze(2).to_broadcast([128, 16, 2]), op=ALU.add)
                self.I("dve", "tensor_scalar", r=["mod"], w=["tmpm"], out=tmpm[:], in0=mod[:, 8:16, :],
                       scalar1=1.0, scalar2=None, op0=ALU.add)
                self.I("dve", "tensor_tensor", r=["tmpm", "ngt"], w=["A_fm"],
                       out=self.A_fm[:, l].rearrange("p j k -> p k j"), in0=tmpm[:],
                       in1=ngt[:, l, :].unsqueeze(2).to_broadcast([128, 8, 2]), op=ALU.mult)
                self.I("dve", "tensor_copy", r=["mod"], w=["B_fm"],
                       out=self.B_fm[:, l].rearrange("p j k -> p k j"), in_=mod[:, 0:8, :])
                for j in range(2):
                    if l == 1 and j == 1:
                        continue
                    for n2 in range(2):
                        for kc in range(8):
                            self.I("pe", "matmul", r=[wk[kc], "scb"], w=["ps_g"], out=ps_g[:, n2 * 512:(n2 + 1) * 512],
                                   lhsT=scb[:, kc * 2 + j, :], rhs=wada[:, kc, 2 * DM + n2 * 512:2 * DM + (n2 + 1) * 512],
                                   start=(kc == 0), stop=(kc == 7))
                    self.I("dve", "tensor_tensor", r=["ps_g", "bbc"], w=["gate_bc%d%d" % (l, j)],
                           out=self.gate_bc[l][j][:], in0=ps_g[:], in1=bbc[:], op=ALU.add)
            S.barrier()

    def phase_A(self, l, ph):
        nc, S = self.nc, self.S
        if True:
            T, P = self.TT(ph)
            NB = 4
            xt = [T("xt%d" % i, [128, DM], F32) for i in range(NB)]
            xn = [T("xn%d" % i, [128, DM], F32) for i in range(3)]
            junk = T("junkA", [128, DM], BF16)
            ss = [T("ssA%d" % i, [128, 1], F32) for i in range(3)]
            rs = [T("rsA%d" % i, [128, 1], F32) for i in range(3)]
            pt = [P("ptA%d" % i, [128, 8, 128], F32) for i in range(2)]
            xsrc = self.x_in if l == 0 else self.xs
            csrc = self.ctx_in if l == 0 else self.cs
            def a_ld(t):
                if t >= NTL:
                    return
                b3 = t % NB
                src = csrc[t * 128:(t + 1) * 128, :] if t < 2 else xsrc[(t - 2) * 128:(t - 1) * 128, :]
                S.dma("act", xt[b3][:], src, writes=["xt%d" % b3])

            def a_s1(t):
                b3, b2 = t % NB, t % 3
                if t == 0:
                    a_ld(0)
                a_ld(t + 1)
                self.I("act", "activation", r=["xt%d" % b3], w=["junkA", "ssA%d" % b2], out=junk[:], in_=xt[b3][:],
                       func=AF.Square, accum_out=ss[b2][:])
                self.I("act", "activation", r=["ssA%d" % b2, "eps_t"], w=["rsA%d" % b2], out=rs[b2][:], in_=ss[b2][:],
                       func=AF.Ln, scale=1.0 / DM, bias=self.eps_t[:, 0:1])
                self.I("act", "activation", r=["rsA%d" % b2], w=["rsA%d" % b2], out=rs[b2][:], in_=rs[b2][:],
                       func=AF.Exp, scale=-0.5)
                self.I("dve", "tensor_scalar", r=["xt%d" % b3, "rsA%d" % b2], w=["xn%d" % b2], out=xn[b2][:], in0=xt[b3][:],
                       scalar1=rs[b2][:, 0:1], scalar2=None, op0=ALU.mult)

            def a_s2(t):
                xb = t % 3
                b2 = t % 2
                j = 1 if t < 2 else 0
                for kc in range(8):
                    self.I("pe", "transpose", r=["xn%d" % xb, "identf"], w=["ptA%d" % b2], out=pt[b2][:, kc, :],
                           in_=xn[xb][:, kc * 128:(kc + 1) * 128], identity=self.identf[:])
                for kc in range(8):
                    o = self.hT[:, kc, t * 128:(t + 1) * 128]
                    a_ = self.A_fm[:, l, j, kc:kc + 1]
                    b_ = self.B_fm[:, l, j, kc:kc + 1]
                    if kc != 0:
                        self.I("dve", "tensor_scalar", r=["ptA%d" % b2, "A_fm", "B_fm"], w=["hT%d" % t], out=o,
                               in0=pt[b2][:, kc, :], scalar1=a_, scalar2=b_, op0=ALU.mult, op1=ALU.add)
                    else:
                        self.I("act", "activation", r=["ptA%d" % b2, "A_fm", "B_fm"], w=["hT%d" % t], out=o,
                               in_=pt[b2][:, kc, :], func=AF.Identity, scale=a_, bias=b_)

            a_s1(0)
            a_s1(1)

            def step(t):
                if t + 2 < NTL:
                    a_s1(t + 2)
                a_s2(t)
            return step

    def phase_B(self, l):
        nc, S = self.nc, self.S
        groups = []
        def tm(dst, c0, dc0, n, fn=None):
            groups.append(("TM", dst, c0, dc0, n, fn))
        def fm(dst, c0, dr0, n, fn=None, dt=BF16):
            groups.append(("FM", dst, c0, dr0, n, fn, dt))
        tm(self.GV, C_GV, 0, 512)
        fm(self.GQ, C_GQ, 0, 512, "qscale")
        fm(self.GK, C_GK, 0, 512)
        tm(self.GV, C_GV + 512, 512, 512)
        fm(self.GLR, C_LR, 0, 32, None, F32)
        fm(self.HQ, C_HQ, 0, 512); fm(self.HQ, C_HQ + 512, 512, 512)
        for i in range(4):
            fm(self.HZ, C_HF + i * 512, i * 512, 512, None, F32)
        tm(self.HV, C_HI, 0, 512); tm(self.HV, C_HI + 512, 512, 512)
        tm(self.AQ, C_AQ, 0, 512, "rope"); tm(self.AQ, C_AQ + 512, 512, 512, "rope")
        tm(self.AK, C_AK, 0, 256, "rope")
        tm(self.AV, C_AV, 0, 256)
        fm(self.GG, C_GG, 0, 512, "silu"); fm(self.GG, C_GG + 512, 512, 512, "silu")
        fm(self.HG, C_HG, 0, 512, "silu"); fm(self.HG, C_HG + 512, 512, 512, "silu")
        tm(self.AG, C_AG, 0, 512, "silu"); tm(self.AG, C_AG + 512, 512, 512, "silu")
        for i in range(6):
            fm(self.MG, C_MG + i * 512, i * 512, 512, "sigmoid")
        with ExitStack() as ph:
            a_step = self.phase_A(l, ph)
            T, P = self.TT(ph)
            wb = [T("wbB%d" % i, [128, 8, 512], BF16) for i in range(2)]
            stg = [T("stgB%d" % i, [128, 4, 512], BF16) for i in range(2)]
            stg32 = [T("stgB32_%d" % i, [128, 4, 512], F32) for i in range(2)]
            rope = T("ropeB", [128, 32, 128], F32)
            t1 = T("t1B", [128, 4, 2, 32], F32)
            t2 = T("t2B", [128, 4, 2, 32], F32)
            ps = [P("psB%d" % i, [128, 512], F32) for i in range(4)]
            S.dma("sp", rope[:], self.c_rope.rearrange("(t p) c -> p t c", p=128), writes=["ropeB"])
            pi = 0
            si = 0
            hk = ["hT%d" % t for t in range(NTL)]
            for gi, g in enumerate(groups):
                kind, dst, c0, d0, n = g[0], g[1], g[2], g[3], g[4]
                fn = g[5]
                w = wb[gi % 2]
                wk = "wbB%d" % (gi % 2)
                for half in range(2):
                    S.dma("pool", w[:, half * 4:(half + 1) * 4, 0:n],
                          self.w_in[l, half * 512:(half + 1) * 512, c0:c0 + n].rearrange("(k p) n -> p k n", p=128),
                          writes=[wk])
                if kind == "TM":
                    for t0 in range(0, NTL, 4):
                        nt = min(4, NTL - t0)
                        sb = stg[si % 2]
                        sk = "stgB%d" % (si % 2)
                        si += 1
                        for ti in range(nt):
                            t = t0 + ti
                            p = ps[pi % 4]
                            pk = "psB%d" % (pi % 4)
                            pi += 1
                            if gi == 0:
                                if t == 0:
                                    for t_ in range(min(ALAG, NTL)):
                                        a_step(t_)
                                if t + ALAG < NTL:
                                    a_step(t + ALAG)
                            for kc in range(8):
                                self.I("pe", "matmul", r=[hk[t], wk], w=[pk], out=p[:, 0:n],
                                       lhsT=self.hT[:, kc, t * 128:(t + 1) * 128], rhs=w[:, kc, 0:n],
                                       start=(kc == 0), stop=(kc == 7))
                            o = sb[:, ti, 0:n]
                            if fn == "silu":
                                self.I("act", "activation", r=[pk], w=[sk], out=o, in_=p[:, 0:n], func=AF.Silu)
                            elif fn == "rope" and t >= 2:
                                nh = n // 128
                                pv = p[:, 0:n].rearrange("p (h a b r) -> p h a b r", h=nh, a=2, b=2)
                                ov = o.rearrange("p (h a b r) -> p h a b r", h=nh, a=2, b=2)
                                rv = rope[:, t - 2, :].rearrange("p (s a r) -> p s a r", s=2, a=2)
                                cosb = rv[:, 0].unsqueeze(1).to_broadcast([128, nh, 2, 32])
                                sinb = rv[:, 1].unsqueeze(1).to_broadcast([128, nh, 2, 32])
                                u1 = pv[:, :, :, 0, :]
                                u2 = pv[:, :, :, 1, :]
                                a1 = t1[:, 0:nh]
                                a2 = t2[:, 0:nh]
                                self.I("dve", "tensor_tensor", r=[pk, "ropeB"], w=["t1B"], out=a1, in0=u1, in1=cosb, op=ALU.mult)
                                self.I("dve", "tensor_tensor", r=[pk, "ropeB"], w=["t2B"], out=a2, in0=u2, in1=sinb, op=ALU.mult)
                                self.I("dve", "tensor_tensor", r=["t1B", "t2B"], w=[sk], out=ov[:, :, :, 0, :], in0=a1, in1=a2, op=ALU.subtract)
                                self.I("dve", "tensor_tensor", r=[pk, "ropeB"], w=["t1B"], out=a1, in0=u2, in1=cosb, op=ALU.mult)
                                self.I("dve", "tensor_tensor", r=[pk, "ropeB"], w=["t2B"], out=a2, in0=u1, in1=sinb, op=ALU.mult)
                                self.I("dve", "tensor_tensor", r=["t1B", "t2B"], w=[sk], out=ov[:, :, :, 1, :], in0=a1, in1=a2, op=ALU.add)
                            else:
                                self.copy(self.alt(), o, p[:, 0:n], [pk], [sk])
                        S.dma("sp", dst[t0 * 128:(t0 + nt) * 128, d0:d0 + n].rearrange("(t p) n -> p t n", p=128),
                              sb[:, 0:nt, 0:n], reads=[sk])
                else:
                    dt = g[6]
                    nf = max(1, n // 128)
                    m = min(n, 128)
                    for tt in range(9):
                        wtok = 512 if tt < 8 else 256
                        sb = (stg if dt == BF16 else stg32)[si % 2]
                        sk = ("stgB%d" if dt == BF16 else "stgB32_%d") % (si % 2)
                        si += 1
                        for f in range(nf):
                            p = ps[pi % 4]
                            pk = "psB%d" % (pi % 4)
                            pi += 1
                            for kc in range(8):
                                self.I("pe", "matmul", r=hk[tt * 4:tt * 4 + 4] + [wk], w=[pk], out=p[0:m, 0:wtok],
                                       lhsT=w[:, kc, f * 128:f * 128 + m], rhs=self.hT[:, kc, tt * 512:tt * 512 + wtok],
                                       start=(kc == 0), stop=(kc == 7))
                            o = sb[0:m, f, 0:wtok]
                            if fn == "sigmoid":
                                self.I("act", "activation", r=[pk], w=[sk], out=o, in_=p[0:m, 0:wtok], func=AF.Sigmoid)
                            elif fn == "silu":
                                self.I("act", "activation", r=[pk], w=[sk], out=o, in_=p[0:m, 0:wtok], func=AF.Silu)
                            elif fn == "qscale":
                                if self.alt() == "act":
                                    self.I("act", "activation", r=[pk], w=[sk], out=o, in_=p[0:m, 0:wtok], func=AF.Copy, scale=128.0 ** -0.5)
                                else:
                                    self.I("dve", "tensor_scalar", r=[pk], w=[sk], out=o, in0=p[0:m, 0:wtok], scalar1=128.0 ** -0.5,
                                           scalar2=None, op0=ALU.mult)
                            else:
                                self.copy(self.alt(), o, p[0:m, 0:wtok], [pk], [sk])
                        if n >= 128:
                            S.dma("sp", dst[d0:d0 + n, tt * 512:tt * 512 + wtok].rearrange("(f p) t -> p f t", p=128),
                                  sb[:, 0:nf, 0:wtok], reads=[sk])
                        else:
                            S.dma("sp", dst[d0:d0 + n, tt * 512:tt * 512 + wtok], sb[0:m, 0, 0:wtok], reads=[sk])
            S.barrier()

    def phase_rec(self, l, mixer):
        nc, S = self.nc, self.S
        gla = mixer == "gla"
        bst = not gla
        H = 4 if gla else 8
        V = 256 if gla else 128
        Lc = 128 if gla else 32
        NCH = 128 // Lc
        HT = H * 128
        S2 = H * NCH
        scl = (-1.0 / 16.0) if gla else (-1.0 if l == 0 else 1.0)
        Qd = self.GQ if gla else self.HQ
        Vd = self.GV if gla else self.HV
        Gd = self.GG if gla else self.HG
        OF = self.OFG if gla else self.OFH
        Y = self.YG if gla else self.YH
        ngd = self.gla_ng if gla else self.hg_ng
        with ExitStack() as ph:
            T, P = self.TT(ph)
            qt = [T("qt%d" % i, [128, H, 128], BF16) for i in range(3)]
            vt = [T("vt%d" % i, [Lc, NCH, DM], BF16) for i in range(3)]
            gt = [T("gt%d" % i, [128, 8, 128], BF16) for i in range(3)]
            oft = [T("oft%d" % i, [128, 8, 128], F32) for i in range(3)]
            if gla:
                kt = [T("kt%d" % i, [128, H, 128], BF16) for i in range(3)]
                lrt = [T("lrt%d" % i, [33, 128], F32) for i in range(3)]
                wa2t = T("wa2t", [33, 512], F32)
                ps_z = P("ps_z", [128, H, 128], F32)
            else:
                zt = [T("zt%d" % i, [128, H, 128], F32) for i in range(3)]
                ktc = [T("ktc%d" % i, [128, H, 128], BF16) for i in range(4)]
                ff = T("ff", [128, H, 128], F32)
                lbr = T("lbr", [128, 2, 2, 8], F32)
                lbv = T("lbv", [128, 2, 8], F32)
                oml = T("oml", [128, 2, 8], F32)
            et = T("et", [128, HT], F32)
            xs = T("xs", [128, HT], F32)
            ones = T("ones", [128, HT], F32)
            Pext = T("Pext", [128, HT + 1], F32)
            Dm = T("Dm", [128, S2, Lc], F32)
            Eq = T("Eq", [128, HT], F32)
            Ek = T("Ek", [128, HT], F32)
            facin = T("facin", [128, 3, S2], F32)
            fac = [T("fac%d" % i, [128, 3, S2], F32) for i in range(4)]
            qtl = [T("qtl%d" % i, [128, H, 128], BF16) for i in range(3)]
            qsl = [T("qsl%d" % i, [128, H, 128], BF16) for i in range(3)]
            ktl = [T("ktl%d" % i, [128, H, 128], BF16) for i in range(3)]
            kul = [T("kul%d" % i, [128, H, 128], BF16) for i in range(3)]
            attm = [T("attm%d" % i, [Lc, H, Lc], BF16) for i in range(2)]
            kTs = [T("kTs%d" % i, [Lc, H, 128], BF16) for i in range(2)]
            Sb = [T("Sb%d" % i, [128, H, V], BF16) for i in range(2)]
            St = [T("St%d" % i, [128, H, V], F32) for i in range(2)] if not bst else None
            facb = [T("facb%d" % i, [128, S2], BF16) for i in range(4)] if bst else None
            t2b = T("t2b", [128, H, V], BF16) if bst else None
            t2 = T("t2", [128, H, V], F32) if not bst else None
            VB = V // 128
            osb = T("osb", [128, 8, 128], F32)
            sqb = T("sqb", [128, 8, 128], BF16)
            rst = T("rst", [128, H, 128], F32)
            onesb = T("onesb", [128, 128], BF16)
            yTs = [T("yTs%d" % i, [128, 8, 512], BF16) for i in range(2)]
            gain = T("gain", [128, 8], F32)
            ps_att = P("ps_att", [Lc, H, Lc], F32)
            ps_kT = P("ps_kT", [Lc, H, 128], BF16)
            ps_U = P("ps_U", [128, H, V], F32)
            ps_o = P("ps_o", [128, 8, 128], F32)
            ps_ss = P("ps_ss", [128, H, 128], F32)

            mfull = [T("mfull%d" % i, [Lc, H, Lc], mybir.dt.uint32) for i in range(2)]
            for i, (mm, mkk) in enumerate(((self.mle_u, "mle_u"), (self.mge_u, "mge_u"))):
                self.I("dve", "tensor_copy", r=[mkk], w=["mfull%d" % i], out=mfull[i][:],
                       in_=mm[0:Lc, 0:Lc].unsqueeze(1).to_broadcast([Lc, H, Lc]))
            self.I("pool", "memset", w=["ones"], ap=ones[:], constant=1.0)
            self.I("pool", "memset", w=["Pext"], ap=Pext[:, 0:1], constant=0.0)
            S.dma("sp", gain[:], ngd[:, l, :], writes=["gain"])
            self.I("pool", "memset", w=["onesb"], ap=onesb[:], constant=1.0)
            if gla:
                for i in range(3):
                    self.I("pool", "memset", w=["lrt%d" % i], ap=lrt[i][32:33, :], constant=1.0)
            elif l == 1:
                S.dma("sp", lbr[:], self.lb_fm, writes=["lbr"])
                self.I("dve", "tensor_tensor", r=["lbr"], w=["lbv"], out=lbv[:], in0=lbr[:, 1], in1=lbr[:, 0], op=ALU.subtract)
                self.I("act", "activation", r=["lbv"], w=["lbv"], out=lbv[:], in_=lbv[:], func=AF.Exp, scale=-1.0)
                self.I("dve", "tensor_scalar", r=["lbv"], w=["lbv"], out=lbv[:], in0=lbv[:], scalar1=1.0, scalar2=None, op0=ALU.add)
                self.I("dve", "reciprocal", r=["lbv"], w=["lbv"], out=lbv[:], in_=lbv[:])
                self.I("dve", "tensor_scalar", r=["lbv"], w=["oml"], out=oml[:], in0=lbv[:], scalar1=-1.0, scalar2=1.0,
                       op0=ALU.mult, op1=ALU.add)

            tile_it = [0]
            grp_cnt = {}
            for d in range(2):
                fwd = d == 0
                tiles = list(range(NTL)) if fwd else [1, 0] + list(range(NTL - 1, 1, -1))
                chunks = list(range(NCH)) if fwd else list(range(NCH - 1, -1, -1))
                mask = (self.mle_u if fwd else self.mge_u)
                mk = "mle_u" if fwd else "mge_u"
                sq_sign = scl if fwd else -scl
                i_ss, i_us = (0, 1) if fwd else (1, 0)
                if bst:
                    self.I("pool", "memset", w=["Sb1"], ap=Sb[1][:], constant=0.0)
                else:
                    self.I("pool", "memset", w=["St1"], ap=St[1][:], constant=0.0)
                for i in range(2):
                    self.I("pool", "memset", w=["attm%d" % i], ap=attm[i][:], constant=0.0)
                if gla:
                    S.dma("sp", wa2t[:], self.wa2p[l, d], writes=["wa2t"])
                tinfo = {}

                def prep_stages(t):
                    n = tile_it[0]
                    tile_it[0] += 1
                    tinfo[t] = n
                    tb = n % 3
                    zb = n % 2
                    fb = n % 4
                    tok = slice(t * 128, (t + 1) * 128)
                    need_y = (not fwd) and (t >= 2 or l == 0)
                    fk = "fac%d" % fb
                    ksrc = kt[tb] if gla else ktc[fb]
                    kkey = ("kt%d" % tb) if gla else ("ktc%d" % fb)
                    Pv0 = Pext[:, 0:HT].rearrange("p (s i) -> p s i", i=Lc)
                    Pv1 = Pext[:, 1:1 + HT].rearrange("p (s i) -> p s i", i=Lc)
                    ref = Pv0[:, :, Lc // 2:Lc // 2 + 1]
                    st0 = Pv0[:, :, 0:1]
                    en0 = Pv1[:, :, Lc - 1:Lc]
                    qf = qt[tb][:].rearrange("p h t -> p (h t)")

                    def sm1():
                        if gla:
                            S.dma("sp", lrt[zb][0:32, :], self.GLR[:, tok], writes=["lrt%d" % zb])
                        else:
                            S.dma("sp", zt[zb][:], self.HZ[d * 1024:(d + 1) * 1024, tok].rearrange("(h p) t -> p h t", p=128),
                                  writes=["zt%d" % zb])

                    def s0():
                        if gla:
                            for h in range(H):
                                self.I("pe", "matmul", r=["wa2t", "lrt%d" % zb], w=["ps_z"], out=ps_z[:, h, :],
                                       lhsT=wa2t[:, h * 128:(h + 1) * 128], rhs=lrt[zb][:, :], start=True, stop=True)
                            self.I("act", "activation", r=["ps_z"], w=["et"], out=et[:], in_=ps_z[:].rearrange("p h t -> p (h t)"),
                                   func=AF.Exp, scale=-1.0)
                        else:
                            zf = zt[zb][:].rearrange("p h t -> p (h t)")
                            self.I("act", "activation", r=["zt%d" % zb], w=["et"], out=et[:], in_=zf, func=AF.Exp, scale=-1.0)
                        self.I("act", "activation", r=["et", "one_t"], w=["xs"], out=xs[:], in_=et[:], func=AF.Ln,
                               bias=self.one_t[:, 0:1], scale=1.0)

                    def s1():
                        if not gla:
                            fv = ff[:].rearrange("p h t -> p (h t)")
                            self.I("act", "activation", r=["xs"], w=["ff"], out=fv, in_=xs[:], func=AF.Exp, scale=-1.0)
                            if l == 1:
                                self.I("dve", "tensor_tensor", r=["ff", "oml"], w=["ff"], out=ff[:], in0=ff[:],
                                       in1=oml[:, d, :].unsqueeze(2).to_broadcast([128, H, 128]), op=ALU.mult)
                                self.I("dve", "tensor_tensor", r=["ff", "lbv"], w=["ff"], out=ff[:], in0=ff[:],
                                       in1=lbv[:, d, :].unsqueeze(2).to_broadcast([128, H, 128]), op=ALU.add)
                                self.I("act", "activation", r=["ff"], w=["xs"], out=xs[:], in_=fv, func=AF.Ln)
                            self.I("act", "activation", r=["ff", "one_t"], w=[kkey], out=ktc[fb][:].rearrange("p h t -> p (h t)"), in_=fv,
                                   func=AF.Identity, scale=-1.0, bias=self.one_t[:, 0:1])
                        self.I("dve", "tensor_tensor_scan", r=["ones", "xs", "Pext"], w=["Pext"], out=Pext[:, 1:1 + HT], data0=ones[:],
                               data1=xs[:], initial=0.0, op0=ALU.mult, op1=ALU.add)

                    def s2():
                        self.I("dve", "tensor_tensor", r=["Pext"], w=["Dm"], out=Dm[:], in0=(Pv1 if fwd else Pv0),
                               in1=ref.to_broadcast([128, S2, Lc]), op=ALU.subtract)
                        self.I("dve", "tensor_tensor", r=["Pext"], w=["facin"], out=facin[:, 0, :].unsqueeze(2), in0=ref, in1=st0, op=ALU.subtract)
                        self.I("dve", "tensor_tensor", r=["Pext"], w=["facin"], out=facin[:, 2, :].unsqueeze(2), in0=en0, in1=st0, op=ALU.subtract)
                        self.I("dve", "tensor_tensor", r=["facin"], w=["facin"], out=facin[:, 1, :], in0=facin[:, 2, :], in1=facin[:, 0, :], op=ALU.subtract)

                    def s3():
                        S.dma("sp", qt[tb][:], Qd[:, tok].rearrange("(h p) t -> p h t", p=128), writes=["qt%d" % tb])
                        if gla:
                            S.dma("sp", kt[tb][:], self.GK[:, tok].rearrange("(h p) t -> p h t", p=128), writes=["kt%d" % tb])
                        self.I("act", "activation", r=["facin"], w=[fk], out=fac[fb][:], in_=facin[:], func=AF.Exp, scale=scl)
                        if bst:
                            self.I("act", "activation", r=["facin"], w=["facb%d" % fb], out=facb[fb][:], in_=facin[:, 2, :], func=AF.Exp, scale=scl)
                        Df = Dm[:].rearrange("p s i -> p (s i)")
                        self.I("act", "activation", r=["Dm"], w=["Eq"], out=Eq[:], in_=Df, func=AF.Exp, scale=sq_sign)
                        self.I("act", "activation", r=["Dm"], w=["Ek"], out=Ek[:], in_=Df, func=AF.Exp, scale=-sq_sign)

                    def s4l():
                        S.dma("sp", vt[tb][:], Vd[tok, :].rearrange("(c j) f -> j c f", j=Lc), writes=["vt%d" % tb])
                        if need_y:
                            S.dma("sp", oft[tb][:], OF[:, tok].rearrange("(f p) t -> p f t", p=128), reads=["OF%d" % t], writes=["oft%d" % tb])
                            S.dma("sp", gt[tb][:], Gd[:, tok].rearrange("(f p) t -> p f t", p=128), writes=["gt%d" % tb])

                    def s4a():
                        self.I("dve", "tensor_tensor", r=["qt%d" % tb, "Eq"], w=["qtl%d" % tb], out=qtl[tb][:].rearrange("p h t -> p (h t)"),
                               in0=qf, in1=Eq[:], op=ALU.mult)
                        self.I("dve", "tensor_tensor", r=[kkey, "Ek"], w=["ktl%d" % tb], out=ktl[tb][:].rearrange("p h t -> p (h t)"),
                               in0=ksrc[:].rearrange("p h t -> p (h t)"), in1=Ek[:], op=ALU.mult)


                    def s4b():
                        self.I("dve", "tensor_tensor", r=["ktl%d" % tb, fk], w=["kul%d" % tb],
                               out=kul[tb][:].rearrange("p h (c i) -> p (h c) i", i=Lc),
                               in0=ktl[tb][:].rearrange("p h (c i) -> p (h c) i", i=Lc),
                               in1=fac[fb][:, i_us, :].unsqueeze(2).to_broadcast([128, S2, Lc]), op=ALU.mult)

                    def s5():
                        self.I("dve", "tensor_tensor", r=["qtl%d" % tb, fk], w=["qsl%d" % tb],
                               out=qsl[tb][:].rearrange("p h (c i) -> p (h c) i", i=Lc),
                               in0=qtl[tb][:].rearrange("p h (c i) -> p (h c) i", i=Lc),
                               in1=fac[fb][:, i_ss, :].unsqueeze(2).to_broadcast([128, S2, Lc]), op=ALU.mult)

                    return [sm1, s0, s1, s2, s3, s4a, s4b, s5, s4l]

                seq = [(t, c) for t in tiles for c in chunks]
                G = len(seq)

                def fsel(fb_, r_, c):
                    return fac[fb_][:, r_, :].rearrange("p (h c) -> p h c", c=NCH)[:, :, c:c + 1].to_broadcast([128, H, V])

                def stA(g):
                    t, c = seq[g]
                    tb = tinfo[t] % 3
                    gb = g % 2
                    cs_ = slice(c * Lc, (c + 1) * Lc)
                    for h in range(H):
                        self.I("pe", "matmul", r=["ktl%d" % tb, "qtl%d" % tb], w=["ps_att"], out=ps_att[:, h, :],
                               lhsT=ktl[tb][:, h, cs_], rhs=qtl[tb][:, h, cs_], start=True, stop=True)
                    for h in range(H):
                        self.I("pe", "transpose", r=["kul%d" % tb, "identb"], w=["ps_kT"], out=ps_kT[:, h, :],
                               in_=kul[tb][:, h, cs_], identity=self.identb[:])
                    self.I("act", "activation", r=["ps_kT"], w=["kTs%d" % gb], out=kTs[gb][:], in_=ps_kT[:], func=AF.Copy)

                def stAm(g):
                    t, c = seq[g]
                    tb = tinfo[t] % 3
                    gb = g % 2
                    self.I("dve", "copy_predicated", r=["ps_att", "mfull%d" % d, "attm%d" % gb], w=["attm%d" % gb],
                           out=attm[gb][:].rearrange("p h i -> p (h i)"), mask=mfull[d][:].rearrange("p h i -> p (h i)"),
                           data=ps_att[:].rearrange("p h i -> p (h i)"))

                def stB(g):
                    t, c = seq[g]
                    tb = tinfo[t] % 3
                    fb_ = tinfo[t] % 4
                    gb = g % 2
                    for h in range(H):
                        vs = slice(h * V, (h + 1) * V)
                        self.I("pe", "matmul", r=["kTs%d" % gb, "vt%d" % tb], w=["ps_U"], out=ps_U[:, h, :],
                               lhsT=kTs[gb][:, h, :], rhs=vt[tb][:, c, vs], start=True, stop=True)

                def stSB(g):
                    if bst:
                        return
                    gb = g % 2
                    pb = (g - 1) % 2
                    self.I("act", "activation", r=["St%d" % pb], w=["Sb%d" % gb], out=Sb[gb][:], in_=St[pb][:], func=AF.Copy)

                def stC(g):
                    t, c = seq[g]
                    tb = tinfo[t] % 3
                    gb = g % 2
                    sbi = ((g - 1) % 2) if bst else gb
                    cs_ = slice(c * Lc, (c + 1) * Lc)
                    for h in range(H):
                        for vb in range(VB):
                            fb2 = h * VB + vb
                            self.I("pe", "matmul", r=["attm%d" % gb, "vt%d" % tb], w=["ps_o"], out=ps_o[:, fb2, cs_],
                                   lhsT=vt[tb][:, c, fb2 * 128:(fb2 + 1) * 128], rhs=attm[gb][:, h, :], start=True, stop=False)
                            self.I("pe", "matmul", r=["qsl%d" % tb, "Sb%d" % sbi], w=["ps_o"], out=ps_o[:, fb2, cs_],
                                   lhsT=Sb[sbi][:, h, vb * 128:(vb + 1) * 128], rhs=qsl[tb][:, h, cs_], start=False, stop=True)

                def stCH(g):
                    t, c = seq[g]
                    fb_ = tinfo[t] % 4
                    gb = g % 2
                    pb = (g - 1) % 2
                    if bst:
                        dsel = facb[fb_][:].rearrange("p (h c) -> p h c", c=NCH)[:, :, c:c + 1].to_broadcast([128, H, V])
                        self.I("dve", "tensor_tensor", r=["Sb%d" % pb, "facb%d" % fb_], w=["t2b"], out=t2b[:], in0=Sb[pb][:], in1=dsel, op=ALU.mult)
                        self.I("dve", "tensor_tensor", r=["t2b", "ps_U"], w=["Sb%d" % gb], out=Sb[gb][:], in0=ps_U[:], in1=t2b[:], op=ALU.add)
                        return
                    self.I("dve", "tensor_tensor", r=["St%d" % pb, "fac%d" % fb_], w=["t2"], out=t2[:], in0=St[pb][:], in1=fsel(fb_, 2, c), op=ALU.mult)
                    self.I("dve", "tensor_tensor", r=["t2", "ps_U"], w=["St%d" % gb], out=St[gb][:], in0=ps_U[:], in1=t2[:], op=ALU.add)

                def stE(t):
                    tb = tinfo[t] % 3
                    tok = slice(t * 128, (t + 1) * 128)
                    need_y = (not fwd) and (t >= 2 or l == 0)
                    if fwd:
                        self.I("act", "activation", r=["ps_o"], w=["osb"], out=osb[:], in_=ps_o[:], func=AF.Copy)
                        S.dma("pool", OF[:, tok].rearrange("(f p) t -> p f t", p=128), osb[:], reads=["osb"], writes=["OF%d" % t])
                        return
                    if not need_y:
                        return
                    self.I("dve", "tensor_tensor", r=["ps_o", "oft%d" % tb], w=["osb"], out=osb[:], in0=ps_o[:], in1=oft[tb][:], op=ALU.add)
                    self.I("act", "activation", r=["osb"], w=["sqb"], out=sqb[:], in_=osb[:], func=AF.Square)

                    def e_ss():
                        if VB == 1:
                            sf = sqb[:].rearrange("p f t -> p (f t)")
                            pf = ps_ss[:].rearrange("p f t -> p (f t)")
                            for hh in range(2):
                                self.I("pe", "matmul", r=["onesb", "sqb"], w=["ps_ss"], out=pf[:, hh * 512:(hh + 1) * 512],
                                       lhsT=onesb[:], rhs=sf[:, hh * 512:(hh + 1) * 512], start=True, stop=True)
                        else:
                            for h in range(H):
                                for vb in range(VB):
                                    self.I("pe", "matmul", r=["onesb", "sqb"], w=["ps_ss"], out=ps_ss[:, h, :],
                                           lhsT=onesb[:], rhs=sqb[:, h * VB + vb, :], start=(vb == 0), stop=(vb == VB - 1))
                        self.I("act", "activation", r=["ps_ss", "eps_t"], w=["rst"], out=rst[:], in_=ps_ss[:], func=AF.Ln, scale=1.0 / V,
                               bias=self.eps_t[:, 0:1])
                        self.I("act", "activation", r=["rst"], w=["rst"], out=rst[:], in_=rst[:], func=AF.Exp, scale=-0.5)

                    def e1b():
                        if VB == 1:
                            self.I("dve", "scalar_tensor_tensor", r=["osb", "rst", "gain"], w=["osb"], out=osb[:], in0=osb[:],
                                   scalar=gain[:, 0:1], in1=rst[:], op0=ALU.mult, op1=ALU.mult)
                        else:
                            ov = osb[:].rearrange("p (h v) t -> p h v t", v=VB)
                            for vb in range(VB):
                                self.I("dve", "scalar_tensor_tensor", r=["osb", "rst", "gain"], w=["osb"], out=ov[:, :, vb, :],
                                       in0=ov[:, :, vb, :], scalar=gain[:, vb:vb + 1], in1=rst[:], op0=ALU.mult, op1=ALU.mult)

                    def e1c():
                        if t >= 2:
                            g_ = (t - 2) // 4
                            pos = (t - 2) % 4
                            tok0 = 256 + g_ * 512
                            W = 512
                        else:
                            g_ = -1
                            pos = t
                            tok0 = 0
                            W = 256
                        sbuf_y = yTs[g_ % 2]
                        yk = "yTs%d" % (g_ % 2)
                        self.I("dve", "tensor_tensor", r=["osb", "gt%d" % tb], w=[yk], out=sbuf_y[:, :, pos * 128:(pos + 1) * 128],
                               in0=osb[:], in1=gt[tb][:], op=ALU.mult)
                        grp_cnt[g_] = grp_cnt.get(g_, 0) + 1
                        if grp_cnt[g_] == W // 128:
                            S.dma("pool", Y[:, tok0:tok0 + W].rearrange("(f p) t -> p f t", p=128), sbuf_y[:, :, 0:W], reads=[yk])
                            grp_cnt[g_] = 0

                    later.extend([e_ss, e1b, e1c])

                later = []
                NT_ = len(tiles)
                stage_lists = {}

                def emit_stage(k, n_):
                    if 0 <= n_ < NT_:
                        if n_ not in stage_lists:
                            stage_lists[n_] = prep_stages(tiles[n_])
                        stage_lists[n_][k]()

                def slot_items(m):
                    return [(7, m + 1), (5, m + 2), (6, m + 2), (4, m + 3), (3, m + 4), (2, m + 5), (1, m + 6), (0, m + 7), (8, m + 2)]

                def slot_split(m, j):
                    if NCH == 1:
                        return slot_items(m)
                    if j == 0:
                        return [(5, m + 2)]
                    if j == 1:
                        return [(6, m + 2), (4, m + 3), (3, m + 4)]
                    if j == 2:
                        return [(2, m + 5), (1, m + 6)]
                    return [(7, m + 1), (8, m + 2), (0, m + 7)]

                for n_ in range(NT_):
                    stage_lists[n_] = prep_stages(tiles[n_])
                for m in range(-7, 0):
                    for (k, n_) in slot_items(m):
                        emit_stage(k, n_)
                stA(0)
                stAm(0)
                stB(0)
                stSB(0)
                per_slot = -(-7 // NCH)
                for g in range(G):
                    t, c = seq[g]
                    m = g // NCH
                    j = g % NCH
                    last_of_tile = (j == NCH - 1)
                    if g + 1 < G:
                        stA(g + 1)
                    stC(g)
                    stCH(g)
                    if g + 1 < G:
                        stAm(g + 1)
                        stB(g + 1)
                        stSB(g + 1)
                    for _ in range(3 if NCH == 1 else 1):
                        if later:
                            later.pop(0)()
                    if last_of_tile:
                        stE(t)
                    for (k, n_) in slot_split(m, j):
                        emit_stage(k, n_)
                while later:
                    later.pop(0)()
                if d == 1:
                    S.barrier()

    def phase_att(self, l):
        nc, S = self.nc, self.S
        with ExitStack() as ph:
            T, P = self.TT(ph)
            kT_all = T("kT_all", [128, 2, NT], BF16)
            Vall = T("Vall", [128, NTL, 256], BF16)
            onesc = T("onesc", [128, 1], BF16)
            skt = T("skt", [128, 8], F32)
            esink = T("esink", [128, 8], F32)
            akt = [T("akt%d" % i, [128, 256], BF16) for i in range(2)]
            aqt = [T("aqt%d" % i, [128, DM], BF16) for i in range(2)]
            agt = [T("agt%d" % i, [128, DM], BF16) for i in range(2)]
            qT = [T("qT%d" % i, [128, 8, 128], BF16) for i in range(2)]
            ex = [[[T("ex%d_%d_%d" % (bb, k, i), [128, 512], BF16) for i in range(5)] for k in range(2)] for bb in range(2)]
            den = T("den", [128, 8], F32)
            attf = T("attf", [128, 8, 128], F32)
            yb = [T("yb%d" % i, [128, DM], BF16) for i in range(2)]
            yTs = [T("yTs%d" % i, [128, 8, 512], BF16) for i in range(2)]
            ps_kt = P("ps_kt", [128, 2, 128], BF16)
            ps_qT = P("ps_qT", [128, 8, 128], BF16)
            ps_s = [P("ps_s%d" % i, [128, 512], F32) for i in range(2)]
            ps_pv = P("ps_pv", [128, 8, 128], F32)
            ps_rs = P("ps_rs", [128, 8], F32)
            ps_yT = P("ps_yT", [128, 8, 128], BF16)

            self.I("pool", "memset", w=["onesc"], ap=onesc[:], constant=1.0)
            S.dma("sp", skt[:], self.sink[l, :].partition_broadcast(128), writes=["skt"])
            self.I("act", "activation", r=["skt"], w=["esink"], out=esink[:], in_=skt[:], func=AF.Exp)
            S.dma("sp", Vall[:], self.AV.rearrange("(t p) f -> p t f", p=128), writes=["Vall"])
            for t in range(NTL):
                b = t % 2
                S.dma("sp", akt[b][:], self.AK[t * 128:(t + 1) * 128, :], writes=["akt%d" % b])
                for kv in range(2):
                    self.I("pe", "transpose", r=["akt%d" % b, "identb"], w=["ps_kt"], out=ps_kt[:, kv, :],
                           in_=akt[b][:, kv * 128:(kv + 1) * 128], identity=self.identb[:])
                self.copy(self.alt(), kT_all[:, :, t * 128:(t + 1) * 128], ps_kt[:], ["ps_kt"], ["kT_all"])
            blocks = []
            if l == 0:
                blocks += [(0, [(0, None), (1, None)]), (1, [(0, None), (1, None)])]
            for n in range(32):
                keys = [(0, None), (1, None)]
                if n >= 1:
                    keys.append((n + 1, "ge"))
                keys.append((n + 2, None))
                if n <= 30:
                    keys.append((n + 3, "le"))
                blocks.append((n + 2, keys))
            sc_cnt = [0]
            grp_cnt = {}

            def stF(bi):
                t, keys = blocks[bi]
                b = bi % 2
                tok = slice(t * 128, (t + 1) * 128)
                S.dma("sp", aqt[b][:], self.AQ[tok, :], writes=["aqt%d" % b])
                S.dma("sp", agt[b][:], self.AG[tok, :], writes=["agt%d" % b])
                for h in range(8):
                    self.I("pe", "transpose", r=["aqt%d" % b, "identb"], w=["ps_qT"], out=ps_qT[:, h, :],
                           in_=aqt[b][:, h * 128:(h + 1) * 128], identity=self.identb[:])
                self.I("dve", "tensor_copy", r=["ps_qT"], w=["qT%d" % b], out=qT[b][:], in_=ps_qT[:])
                for kv in range(2):
                    for ki, (ktile, mt) in enumerate(keys):
                        p = ps_s[sc_cnt[0] % 2]
                        pk = "ps_s%d" % (sc_cnt[0] % 2)
                        sc_cnt[0] += 1
                        e = ex[b][kv][ki]
                        ek = "ex%d_%d_%d" % (b, kv, ki)
                        self.I("pe", "matmul", r=["kT_all", "qT%d" % b], w=[pk], out=p[:],
                               lhsT=kT_all[:, kv, ktile * 128:(ktile + 1) * 128],
                               rhs=qT[b][:, kv * 4:(kv + 1) * 4, :].rearrange("p h t -> p (h t)"), start=True, stop=True)
                        self.I("act", "activation", r=[pk], w=[ek], out=e[:], in_=p[:], func=AF.Exp, scale=128.0 ** -0.5)
                        if mt is not None:
                            mb = self.mgeb if mt == "ge" else self.mleb
                            ev = e[:].rearrange("p (g i) -> p g i", g=4)
                            self.I("dve", "tensor_tensor", r=[ek, "mgeb", "mleb"], w=[ek], out=ev, in0=ev,
                                   in1=mb[:].unsqueeze(1).to_broadcast([128, 4, 128]), op=ALU.mult)

            def stG1(bi):
                t, keys = blocks[bi]
                b = bi % 2
                nk = len(keys)
                for kv in range(2):
                    for g in range(4):
                        for ki, (ktile, mt) in enumerate(keys):
                            self.I("pe", "matmul", r=["ex%d_%d_%d" % (b, kv, ki), "Vall"], w=["ps_pv"], out=ps_pv[:, kv * 4 + g, :],
                                   lhsT=ex[b][kv][ki][:, g * 128:(g + 1) * 128], rhs=Vall[:, ktile, kv * 128:(kv + 1) * 128],
                                   start=(ki == 0), stop=(ki == nk - 1))
                    for g in range(4):
                        for ki, (ktile, mt) in enumerate(keys):
                            self.I("pe", "matmul", r=["ex%d_%d_%d" % (b, kv, ki), "onesc"], w=["ps_rs"],
                                   out=ps_rs[:, kv * 4 + g:kv * 4 + g + 1],
                                   lhsT=ex[b][kv][ki][:, g * 128:(g + 1) * 128], rhs=onesc[:, 0:1],
                                   start=(ki == 0), stop=(ki == nk - 1))
                self.I("dve", "tensor_tensor", r=["ps_rs", "esink"], w=["den"], out=den[:], in0=ps_rs[:], in1=esink[:], op=ALU.add)
                self.I("dve", "reciprocal", r=["den"], w=["den"], out=den[:], in_=den[:])
                self.I("dve", "tensor_tensor", r=["ps_pv", "den"], w=["attf"], out=attf[:], in0=ps_pv[:],
                       in1=den[:].unsqueeze(2).to_broadcast([128, 8, 128]), op=ALU.mult)
                self.I("dve", "tensor_tensor", r=["attf", "agt%d" % b], w=["yb%d" % b], out=yb[b][:],
                       in0=attf[:].rearrange("p h d -> p (h d)"), in1=agt[b][:], op=ALU.mult)

            def stG2(bi):
                t, keys = blocks[bi]
                b = bi % 2
                for fc in range(8):
                    self.I("pe", "transpose", r=["yb%d" % b, "identb"], w=["ps_yT"], out=ps_yT[:, fc, :],
                           in_=yb[b][:, fc * 128:(fc + 1) * 128], identity=self.identb[:])
                if t >= 2:
                    g_ = (t - 2) // 4
                    pos = (t - 2) % 4
                    tok0 = 256 + g_ * 512
                    W = 512
                else:
                    g_ = -1
                    pos = t
                    tok0 = 0
                    W = 256
                sb = yTs[g_ % 2]
                yk = "yTs%d" % (g_ % 2)
                self.I("act", "activation", r=["ps_yT"], w=[yk], out=sb[:, :, pos * 128:(pos + 1) * 128], in_=ps_yT[:], func=AF.Copy)
                grp_cnt[g_] = grp_cnt.get(g_, 0) + 1
                if grp_cnt[g_] == W // 128:
                    S.dma("pool", self.YA[:, tok0:tok0 + W].rearrange("(f p) t -> p f t", p=128), sb[:, :, 0:W], reads=[yk])

            nb = len(blocks)
            stF(0)
            for bi in range(nb):
                if bi + 1 < nb:
                    stF(bi + 1)
                stG1(bi)
                if bi >= 1:
                    stG2(bi - 1)
            stG2(nb - 1)
            S.barrier()

    def phase_merge(self, l, last):
        nc, S = self.nc, self.S
        with ExitStack() as ph:
            T, P = self.TT(ph)
            wbr, wo = self.wbr, self.wo
            yT = [[T("yTm%d_%d" % (i, n), [128, 8, 512], BF16) for n in range(3)] for i in range(2)]
            mgt = T("mgt", [128, 24, 512], BF16)
            merged = T("merged", [128, 8, 512], BF16)
            mt0 = T("mt0", [128, 512], F32)
            mt1 = T("mt1", [128, 512], F32)
            xt = [T("xtm%d" % i, [128, DM], F32) for i in range(2)]
            xo = [T("xom%d" % i, [128, DM], F32) for i in range(2)]
            fg = T("fg", [128, DM], F32)
            junk = T("junkm", [128, DM], BF16)
            ss = T("ssm", [128, 1], F32)
            rs = T("rsm", [128, 1], F32)
            ps_p = [P("ps_p%d" % i, [128, 512], F32) for i in range(6)]
            ps_out = P("ps_out", [128, DM], F32)
            if last:
                S.dma("sp", fg[:], self.final_g.partition_broadcast(128), writes=["fg"])
            tts = []
            if l == 0:
                tts.append((0, 256, 1))
            for i in range(8):
                tts.append((256 + i * 512, 512, 0))
            Ys = [self.YG, self.YH, self.YA]
            pc = 0
            xc = 0
            for ti, (tok0, W, j) in enumerate(tts):
                b = ti % 2
                for n in range(3):
                    S.dma("sp", yT[b][n][:, :, 0:W], Ys[n][:, tok0:tok0 + W].rearrange("(k p) t -> p k t", p=128),
                          writes=["yTm%d_%d" % (b, n)])
                mgsrc = self.MG[:, tok0:tok0 + W].rearrange("(n d p) t -> p n d t", n=3, d=8)
                mgdst = mgt[:].rearrange("p (n d) t -> p n d t", n=3)
                for dc in range(8):
                    S.dma("sp", mgdst[:, :, dc, 0:W], mgsrc[:, :, dc, :], writes=["mgt%d" % dc])
                for dc in range(8):
                    pp = []
                    for n in range(3):
                        p = ps_p[pc % 6]
                        pk = "ps_p%d" % (pc % 6)
                        pc += 1
                        pp.append((p, pk))
                        for kc in range(8):
                            self.I("pe", "matmul", r=["wbr", "yTm%d_%d" % (b, n)], w=[pk], out=p[:, 0:W],
                                   lhsT=wbr[:, n, kc, dc * 128:(dc + 1) * 128], rhs=yT[b][n][:, kc, 0:W],
                                   start=(kc == 0), stop=(kc == 7))
                    self.I("dve", "tensor_tensor", r=[pp[0][1], "mgt%d" % dc], w=["mt0"], out=mt0[:, 0:W], in0=pp[0][0][:, 0:W],
                           in1=mgt[:, dc, 0:W], op=ALU.mult)
                    self.I("dve", "tensor_tensor", r=[pp[1][1], "mgt%d" % dc], w=["mt1"], out=mt1[:, 0:W], in0=pp[1][0][:, 0:W],
                           in1=mgt[:, 8 + dc, 0:W], op=ALU.mult)
                    self.I("dve", "tensor_tensor", r=["mt0", "mt1"], w=["mt0"], out=mt0[:, 0:W], in0=mt0[:, 0:W], in1=mt1[:, 0:W], op=ALU.add)
                    self.I("dve", "tensor_tensor", r=[pp[2][1], "mgt%d" % dc], w=["mt1"], out=mt1[:, 0:W], in0=pp[2][0][:, 0:W],
                           in1=mgt[:, 16 + dc, 0:W], op=ALU.mult)
                    self.I("dve", "tensor_tensor", r=["mt0", "mt1"], w=["merged"], out=merged[:, dc, 0:W], in0=mt0[:, 0:W], in1=mt1[:, 0:W], op=ALU.add)
                for s_ in range(W // 128):
                    xb = xc % 2
                    xc += 1
                    r0 = tok0 + s_ * 128
                    if j == 1:
                        src = self.ctx_in[r0:r0 + 128, :]
                        dst = self.cs[r0:r0 + 128, :]
                    else:
                        xr = r0 - 256
                        src = (self.x_in if l == 0 else self.xs)[xr:xr + 128, :]
                        dst = (self.out if last else self.xs)[xr:xr + 128, :]
                    S.dma("sp", xt[xb][:], src, writes=["xtm%d" % xb])
                    for n2 in range(2):
                        for kc in range(8):
                            self.I("pe", "matmul", r=["merged", "wo"], w=["ps_out"], out=ps_out[:, n2 * 512:(n2 + 1) * 512],
                                   lhsT=merged[:, kc, s_ * 128:(s_ + 1) * 128], rhs=wo[:, kc, n2 * 512:(n2 + 1) * 512],
                                   start=(kc == 0), stop=(kc == 7))
                    gk = "gate_bc%d%d" % (l, j)
                    self.I("dve", "tensor_tensor", r=["ps_out", gk], w=["xom%d" % xb], out=xo[xb][:], in0=ps_out[:],
                           in1=self.gate_bc[l][j][:], op=ALU.mult)
                    self.I("dve", "tensor_tensor", r=["xom%d" % xb, "xtm%d" % xb], w=["xom%d" % xb], out=xo[xb][:], in0=xo[xb][:],
                           in1=xt[xb][:], op=ALU.add)
                    if last:
                        self.I("act", "activation", r=["xom%d" % xb], w=["junkm", "ssm"], out=junk[:], in_=xo[xb][:],
                               func=AF.Square, accum_out=ss[:])
                        self.I("act", "activation", r=["ssm", "eps_t"], w=["rsm"], out=rs[:], in_=ss[:], func=AF.Ln,
                               scale=1.0 / DM, bias=self.eps_t[:, 0:1])
                        self.I("act", "activation", r=["rsm"], w=["rsm"], out=rs[:], in_=rs[:], func=AF.Exp, scale=-0.5)
                        self.I("dve", "scalar_tensor_tensor", r=["xom%d" % xb, "rsm", "fg"], w=["xom%d" % xb], out=xo[xb][:],
                               in0=xo[xb][:], scalar=rs[:, 0:1], in1=fg[:], op0=ALU.mult, op1=ALU.mult)
                    S.dma("pool", dst, xo[xb][:], reads=["xom%d" % xb])
            S.barrier()


def host_inputs(x, c, ctx, c_ctx, norm_g, w_ada, b_ada, w_in, gla_w_a2, gla_b_a2, gla_norm_g,
                hgrn_lb_logits, hgrn_norm_g, attn_sink, w_branch, w_out, final_g):
    f = lambda a: np.ascontiguousarray(np.asarray(a, dtype=np.float32))
    x, c, ctx, c_ctx = f(x), f(c), f(ctx), f(c_ctx)
    shared = {}
    shared["ng_fm"] = f(f(norm_g).reshape(2, 8, 128).transpose(2, 0, 1))
    shared["w_ada"] = f(w_ada)
    shared["bada_fm"] = f(f(b_ada).reshape(2, 24, 128).transpose(2, 0, 1))
    shared["b_ada"] = f(b_ada)
    shared["w_in"] = f(w_in)
    wa2p = np.zeros((2, 2, 33, 512), np.float32)
    wa2 = f(gla_w_a2)
    wa2p[:, 0, 0:16, :] = wa2[:, 0]
    wa2p[:, 1, 16:32, :] = wa2[:, 1]
    wa2p[:, :, 32, :] = f(gla_b_a2)
    shared["wa2p"] = wa2p
    gng = f(gla_norm_g).reshape(2, 2, 128)
    shared["gla_ng_fm"] = f(np.tile(gng.transpose(2, 0, 1)[:, :, None, :], (1, 1, 4, 1)).reshape(128, 2, 8))
    hng = f(hgrn_norm_g)
    shared["hg_ng_fm"] = f(np.tile(hng.T[:, :, None], (1, 1, 8)))
    shared["sink"] = f(attn_sink)
    shared["final_g"] = f(final_g)
    shared["lb_fm"] = f(f(hgrn_lb_logits).reshape(2, 2, 8, 128).transpose(3, 0, 1, 2))
    shared["w_branch"] = f(w_branch)
    shared["w_out"] = f(w_out)
    shared["c_ident"] = np.eye(128, dtype=np.float32)
    jj = np.arange(128)[:, None]
    ii = np.arange(128)[None, :]
    shared["c_mle"] = (jj <= ii).astype(np.float32)
    shared["c_mge"] = (jj >= ii).astype(np.float32)
    shared["c_mle_u"] = (jj <= ii).astype(np.uint32)
    shared["c_mge_u"] = (jj >= ii).astype(np.uint32)
    t = np.arange(4096)
    inv = (10000.0 ** (-np.arange(32, dtype=np.float32) / 32)).astype(np.float32)
    ar = (t // 64).astype(np.float32)[:, None] * inv
    ac = (t % 64).astype(np.float32)[:, None] * inv
    rope = np.concatenate([np.cos(ar), np.cos(ac), np.sin(ar), np.sin(ac)], axis=1).astype(np.float32)
    shared["c_rope"] = f(rope)
    maps = []
    for b in range(x.shape[0]):
        m = dict(shared)
        m["x"] = f(x[b])
        m["ctx"] = f(ctx[b])
        cc = np.stack([c[b], c_ctx], axis=-1).reshape(8, 128, 2).transpose(1, 0, 2)
        m["cc"] = f(cc)
        maps.append(m)
    return maps


_NC_CACHE = {}


def kernel(**inputs):
    maps = host_inputs(**inputs)
    if "nc" not in _NC_CACHE:
        _NC_CACHE["nc"] = Builder().build()
    nc = _NC_CACHE["nc"]
    res = run_bass_kernel_spmd(nc, maps, core_ids=list(range(8)))
    return np.stack([np.asarray(r["out"], dtype=np.float32) for r in res.results], axis=0)
```

```python
import numpy as np
from contextlib import ExitStack
import ml_dtypes
import concourse.bass as bass
import concourse.mybir as mybir
from concourse.bass_utils import run_bass_kernel_spmd

F32 = mybir.dt.float32
BF16 = mybir.dt.bfloat16
AF = mybir.ActivationFunctionType
ALU = mybir.AluOpType
AX = mybir.AxisListType

NDMASEM = 16
ALAG = 3
NT = 4352
NTL = 34
DM = 1024
NIN = 13856
EPS = 1e-6


class Sched:
    COMPUTE = ("pe", "act", "dve", "pool")
    QUEUES = ("sp", "act", "pool")

    def __init__(self, nc, stack):
        self.nc = nc
        self.ops = {e: [] for e in ("pe", "act", "dve", "pool", "sp")}
        self.sem = {}
        self.cnt = {}
        for e in self.COMPUTE:
            self.sem[e] = stack.enter_context(nc.semaphore("s_" + e))
            self.cnt[e] = 0
        self.dsem = {}
        self.dval = {}
        self.dnext = {}
        for q in self.QUEUES:
            self.dsem[q] = [stack.enter_context(nc.semaphore("d_%s%d" % (q, i))) for i in range(NDMASEM)]
            self.dval[q] = [0] * NDMASEM
            self.dnext[q] = 0
        self.waited = {e: {} for e in self.ops}
        self.last_w = {}
        self.readers = {}

    def _deps(self, eng, reads, writes):
        deps = []
        for r in reads:
            t = self.last_w.get(r)
            if t is not None:
                deps.append(t)
        for w in writes:
            t = self.last_w.get(w)
            if t is not None:
                deps.append(t)
            for t in self.readers.get(w, {}).values():
                deps.append(t)
        waits = []
        wd = self.waited[eng]
        best = {}
        for (sem, val, src) in deps:
            if src == "pe" and eng == "pe":
                continue
            k = id(sem)
            if wd.get(k, 0) >= val:
                continue
            if k not in best or best[k][1] < val:
                best[k] = (sem, val)
        for k, (sem, val) in best.items():
            wd[k] = val
            waits.append((sem, val))
        return waits

    def _commit(self, eng, reads, writes, token):
        for r in reads:
            self.readers.setdefault(r, {})[(eng, id(token[0]))] = token
        for w in writes:
            self.last_w[w] = token
            self.readers[w] = {}

    def op(self, eng, fn, reads=(), writes=()):
        waits = self._deps(eng, reads, writes)
        self.cnt[eng] += 1
        token = (self.sem[eng], self.cnt[eng], eng)
        self.ops[eng].append((fn, waits, (self.sem[eng], 1)))
        self._commit(eng, reads, writes, token)

    def dma(self, q, out, in_, reads=(), writes=(), **kw):
        i = self.dnext[q]
        self.dnext[q] = (i + 1) % NDMASEM
        sem = self.dsem[q][i]
        waits = self._deps(q, reads, writes)
        prev = self.dval[q][i]
        if prev > 0 and self.waited[q].get(id(sem), 0) < prev:
            self.waited[q][id(sem)] = prev
            waits.append((sem, prev))
        self.dval[q][i] = prev + 16
        token = (sem, prev + 16, "dma_" + q)

        def fn(e, out=out, in_=in_, kw=kw):
            return e.dma_start(out=out, in_=in_, **kw)

        self.ops[q].append((fn, waits, (sem, 16)))
        self._commit("dma_" + q, reads, writes, token)

    def barrier(self):
        toks = []
        for e in self.COMPUTE:
            if self.cnt[e] > 0:
                toks.append((self.sem[e], self.cnt[e]))
        for q in self.QUEUES:
            for i in range(NDMASEM):
                if self.dval[q][i] > 0:
                    toks.append((self.dsem[q][i], self.dval[q][i]))
        for eng in self.ops:
            waits = []
            for (sem, val) in toks:
                if self.waited[eng].get(id(sem), 0) < val:
                    self.waited[eng][id(sem)] = val
                    waits.append((sem, val))
            if waits:
                self.ops[eng].append((None, waits, None))
        self.last_w = {}
        self.readers = {}

    def emit(self):
        nc = self.nc
        with nc.Block() as block:
            def run(name):
                def body(e):
                    for (fn, waits, inc) in self.ops[name]:
                        for (sem, val) in waits:
                            e.wait_ge(sem, val)
                        if fn is not None:
                            ins = fn(e)
                            ins.then_inc(inc[0], inc[1])
                return body
            block.sync(run("sp"))
            block.tensor(run("pe"))
            block.scalar(run("act"))
            block.vector(run("dve"))
            block.gpsimd(run("pool"))


C_GQ, C_GK, C_GV, C_GG, C_LR = 0, 512, 1024, 2048, 3072
C_HQ, C_HF, C_HI, C_HG = 3104, 4128, 6176, 7200
C_AQ, C_AK, C_AV, C_AG, C_MG = 8224, 9248, 9504, 9760, 10784


class StopBuild(Exception):
    pass


class Builder:
    def __init__(self, nlayers=2, dbg=(), stop=None):
        self.nlayers = nlayers
        self.dbg = set(dbg)
        self.stop = stop
        self.nc = bass.Bass("TRN2", target_bir_lowering=False)
        self.rr = 0

    def din(self, name, shape, dt=F32):
        return self.nc.dram_tensor(name, list(shape), dt, kind="ExternalInput").ap()

    def scr(self, name, shape, dt):
        kind = "ExternalOutput" if name in self.dbg else "Internal"
        return self.nc.dram_tensor(name, list(shape), dt, kind=kind).ap()

    def I(self, eng, name, r=(), w=(), **kw):
        self.S.op(eng, lambda e: getattr(e, name)(**kw), reads=r, writes=w)

    def chk(self, ph, l):
        if self.stop == (ph, l):
            raise StopBuild()

    def TT(self, ph):
        self.uid = getattr(self, "uid", 0) + 1
        u = self.uid
        nc = self.nc
        T = lambda n, s, d: ph.enter_context(nc.sbuf_tensor("%s_u%d" % (n, u), s, d))
        P = lambda n, s, d: ph.enter_context(nc.psum_tensor("%s_u%d" % (n, u), s, d))
        return T, P

    def alt(self):
        self.rr ^= 1
        return "dve" if self.rr else "act"

    def copy(self, eng, out, in_, r, w):
        if eng == "act":
            self.I("act", "activation", r=r, w=w, out=out, in_=in_, func=AF.Copy)
        else:
            self.I(eng, "tensor_copy", r=r, w=w, out=out, in_=in_)

    def build(self):
        nc = self.nc
        L = self.nlayers
        self.x_in = self.din("x", [4096, DM])
        self.ctx_in = self.din("ctx", [256, DM])
        self.cc = self.din("cc", [128, 8, 2])
        self.ng_fm = self.din("ng_fm", [128, 2, 8])
        self.w_ada = self.din("w_ada", [2, DM, 3 * DM])
        self.bada_fm = self.din("bada_fm", [128, 2, 24])
        self.b_ada = self.din("b_ada", [2, 3 * DM])
        self.w_in = self.din("w_in", [2, DM, NIN])
        self.wa2p = self.din("wa2p", [2, 2, 33, 512])
        self.gla_ng = self.din("gla_ng_fm", [128, 2, 8])
        self.hg_ng = self.din("hg_ng_fm", [128, 2, 8])
        self.sink = self.din("sink", [2, 8])
        self.final_g = self.din("final_g", [DM])
        self.lb_fm = self.din("lb_fm", [128, 2, 2, 8])
        self.w_branch = self.din("w_branch", [2, 3, DM, DM])
        self.w_out = self.din("w_out", [2, DM, DM])
        self.c_ident = self.din("c_ident", [128, 128])
        self.c_mle = self.din("c_mle", [128, 128])
        self.c_mge = self.din("c_mge", [128, 128])
        self.c_rope = self.din("c_rope", [4096, 128])
        self.c_mle_u = self.din("c_mle_u", [128, 128], mybir.dt.uint32)
        self.c_mge_u = self.din("c_mge_u", [128, 128], mybir.dt.uint32)
        self.out = nc.dram_tensor("out", [4096, DM], F32, kind="ExternalOutput").ap()

        sc = self.scr
        self.xs = sc("xs", [4096, DM], F32)
        self.cs = sc("cs", [256, DM], F32)
        self.GQ = sc("GQ", [512, NT], BF16)
        self.GK = sc("GK", [512, NT], BF16)
        self.GLR = sc("GLR", [32, NT], F32)
        self.GV = sc("GV", [NT, DM], BF16)
        self.GG = sc("GG", [DM, NT], BF16)
        self.HQ = sc("HQ", [1024, NT], BF16)
        self.HZ = sc("HZ", [2048, NT], F32)
        self.HV = sc("HV", [NT, DM], BF16)
        self.HG = sc("HG", [DM, NT], BF16)
        self.AQ = sc("AQ", [NT, DM], BF16)
        self.AK = sc("AK", [NT, 256], BF16)
        self.AV = sc("AV", [NT, 256], BF16)
        self.AG = sc("AG", [NT, DM], BF16)
        self.MG = sc("MG", [3072, NT], BF16)
        self.OFG = sc("OFG", [DM, NT], F32)
        self.OFH = sc("OFH", [DM, NT], F32)
        self.YG = sc("YG", [DM, NT], BF16)
        self.YH = sc("YH", [DM, NT], BF16)
        self.YA = sc("YA", [DM, NT], BF16)

        with ExitStack() as st:
            self.S = Sched(nc, st)
            S = self.S
            T = lambda n, s, d: st.enter_context(nc.sbuf_tensor(n, s, d))
            self.identf = T("identf", [128, 128], F32)
            self.identb = T("identb", [128, 128], BF16)
            self.mle = T("mle", [128, 128], F32)
            self.mge = T("mge", [128, 128], F32)
            self.mle_u = T("mle_u", [128, 128], mybir.dt.uint32)
            self.mge_u = T("mge_u", [128, 128], mybir.dt.uint32)
            S.dma("sp", self.mle_u[:], self.c_mle_u, writes=["mle_u"])
            S.dma("sp", self.mge_u[:], self.c_mge_u, writes=["mge_u"])
            self.mleb = T("mleb", [128, 128], BF16)
            self.mgeb = T("mgeb", [128, 128], BF16)
            self.A_fm = T("A_fm", [128, 2, 2, 8], F32)
            self.B_fm = T("B_fm", [128, 2, 2, 8], F32)
            self.gate_bc = [[T("gate_bc%d%d" % (l, j), [128, DM], F32) for j in range(2)] for l in range(2)]
            self.eps_t = T("eps_t", [128, 1], F32)
            self.one_t = T("one_t", [128, 1], F32)
            S.dma("sp", self.identf[:], self.c_ident, writes=["identf"])
            S.dma("sp", self.mle[:], self.c_mle, writes=["mle"])
            S.dma("sp", self.mge[:], self.c_mge, writes=["mge"])
            self.I("dve", "tensor_copy", r=["identf"], w=["identb"], out=self.identb[:], in_=self.identf[:])
            self.I("dve", "tensor_copy", r=["mle"], w=["mleb"], out=self.mleb[:], in_=self.mle[:])
            self.I("dve", "tensor_copy", r=["mge"], w=["mgeb"], out=self.mgeb[:], in_=self.mge[:])
            self.I("pool", "memset", w=["eps_t"], ap=self.eps_t[:], constant=EPS)
            self.I("pool", "memset", w=["one_t"], ap=self.one_t[:], constant=1.0)

            self.phase_mod()
            try:
                for l in range(L):
                    with nc.sbuf_tensor("hT_%d" % l, [128, 8, NT], BF16) as hT:
                        self.hT = hT
                        self.phase_B(l)
                    self.chk("B", l)
                    self.phase_rec(l, "gla")
                    self.chk("G", l)
                    self.phase_rec(l, "hgrn")
                    self.chk("H", l)
                    with nc.sbuf_tensor("wbr_%d" % l, [128, 3, 8, DM], BF16) as wbr, \
                            nc.sbuf_tensor("wo_%d" % l, [128, 8, DM], BF16) as wo:
                        self.wbr, self.wo = wbr, wo
                        for n in range(3):
                            for half in range(2):
                                S.dma("pool", wbr[:, n, half * 4:(half + 1) * 4, :],
                                      self.w_branch[l, n, half * 512:(half + 1) * 512, :].rearrange("(k p) d -> p k d", p=128))
                        for half in range(2):
                            S.dma("pool", wo[:, half * 4:(half + 1) * 4, :],
                                  self.w_out[l, half * 512:(half + 1) * 512, :].rearrange("(k p) d -> p k d", p=128))
                        self.phase_att(l)
                        self.phase_merge(l, last=(l == L - 1))
                    self.chk("T", l)
                    self.chk("M", l)
            except StopBuild:
                pass
            S.barrier()
            S.emit()
        return nc

    def phase_mod(self):
        nc, S = self.nc, self.S
        with ExitStack() as ph:
            T, P = self.TT(ph)
            cct = T("cct", [128, 8, 2], F32)
            sct = T("sct", [128, 8, 2], F32)
            scb = T("scb", [128, 16, 128], F32)
            wada = T("wada", [128, 8, 3 * DM], F32)
            ngt = T("ngt", [128, 2, 8], F32)
            bfm = T("bfm", [128, 2, 24], F32)
            bbc = T("bbc", [128, DM], F32)
            mod = T("mod", [128, 16, 2], F32)
            tmpm = T("tmpm", [128, 8, 2], F32)
            ps_m = P("ps_m", [128, 16, 2], F32)
            ps_g = P("ps_g", [128, DM], F32)
            S.dma("sp", cct[:], self.cc, writes=["cct"])
            S.dma("sp", ngt[:], self.ng_fm, writes=["ngt"])
            S.dma("sp", bfm[:], self.bada_fm, writes=["bfm"])
            self.I("act", "activation", r=["cct"], w=["sct"], out=sct[:], in_=cct[:], func=AF.Silu)
            self.I("dve", "tensor_copy", r=["sct"], w=["scb"], out=scb[:],
                   in_=sct[:].rearrange("p k j -> p (k j)").unsqueeze(2).to_broadcast([128, 16, 128]))
            for l in range(self.nlayers):
                for kc in range(8):
                    S.dma("sp", wada[:, kc, :], self.w_ada[l, kc * 128:(kc + 1) * 128, :], writes=["wada%d" % kc])
                S.dma("sp", bbc[:], self.b_ada[l, 2 * DM:3 * DM].partition_broadcast(128), writes=["bbc"])
                wk = ["wada%d" % kc for kc in range(8)]
                for f in range(16):
                    for kc in range(8):
                        self.I("pe", "matmul", r=[wk[kc], "sct"], w=["ps_m"], out=ps_m[:, f, :],
                               lhsT=wada[:, kc, f * 128:(f + 1) * 128], rhs=sct[:, kc, :], start=(kc == 0), stop=(kc == 7))
                self.I("dve", "tensor_tensor", r=["ps_m", "bfm"], w=["mod"], out=mod[:], in0=ps_m[:],
                       in1=bfm[:, l, 0:16].unsqueeze(2).to_broadcast([128, 16, 2]), op=ALU.add)
                self.I("dve", "tensor_scalar", r=["mod"], w=["tmpm"], out=tmpm[:], in0=mod[:, 8:16, :],
                       scalar1=1.0, scalar2=None, op0=ALU.add)
                self.I("dve", "tensor_tensor", r=["tmpm", "ngt"], w=["A_fm"],
                       out=self.A_fm[:, l].rearrange("p j k -> p k j"), in0=tmpm[:],
                       in1=ngt[:, l, :].unsqueeze(2).to_broadcast([128, 8, 2]), op=ALU.mult)
                self.I("dve", "tensor_copy", r=["mod"], w=["B_fm"],
                       out=self.B_fm[:, l].rearrange("p j k -> p k j"), in_=mod[:, 0:8, :])
                for j in range(2):
                    if l == 1 and j == 1:
                        continue
                    for n2 in range(2):
                        for kc in range(8):
                            self.I("pe", "matmul", r=[wk[kc], "scb"], w=["ps_g"], out=ps_g[:, n2 * 512:(n2 + 1) * 512],
                                   lhsT=scb[:, kc * 2 + j, :], rhs=wada[:, kc, 2 * DM + n2 * 512:2 * DM + (n2 + 1) * 512],
                                   start=(kc == 0), stop=(kc == 7))
                    self.I("dve", "tensor_tensor", r=["ps_g", "bbc"], w=["gate_bc%d%d" % (l, j)],
                           out=self.gate_bc[l][j][:], in0=ps_g[:], in1=bbc[:], op=ALU.add)
            S.barrier()

    def phase_A(self, l, ph):
        nc, S = self.nc, self.S
        if True:
            T, P = self.TT(ph)
            NB = 4
            xt = [T("xt%d" % i, [128, DM], F32) for i in range(NB)]
            xn = [T("xn%d" % i, [128, DM], F32) for i in range(3)]
            junk = T("junkA", [128, DM], BF16)
            ss = [T("ssA%d" % i, [128, 1], F32) for i in range(3)]
            rs = [T("rsA%d" % i, [128, 1], F32) for i in range(3)]
            pt = [P("ptA%d" % i, [128, 8, 128], F32) for i in range(2)]
            xsrc = self.x_in if l == 0 else self.xs
            csrc = self.ctx_in if l == 0 else self.cs
            def a_ld(t):
                if t >= NTL:
                    return
                b3 = t % NB
                src = csrc[t * 128:(t + 1) * 128, :] if t < 2 else xsrc[(t - 2) * 128:(t - 1) * 128, :]
                S.dma("act", xt[b3][:], src, writes=["xt%d" % b3])

            def a_s1(t):
                b3, b2 = t % NB, t % 3
                if t == 0:
                    a_ld(0)
                a_ld(t + 1)
                self.I("act", "activation", r=["xt%d" % b3], w=["junkA", "ssA%d" % b2], out=junk[:], in_=xt[b3][:],
                       func=AF.Square, accum_out=ss[b2][:])
                self.I("act", "activation", r=["ssA%d" % b2, "eps_t"], w=["rsA%d" % b2], out=rs[b2][:], in_=ss[b2][:],
                       func=AF.Ln, scale=1.0 / DM, bias=self.eps_t[:, 0:1])
                self.I("act", "activation", r=["rsA%d" % b2], w=["rsA%d" % b2], out=rs[b2][:], in_=rs[b2][:],
                       func=AF.Exp, scale=-0.5)
                self.I("dve", "tensor_scalar", r=["xt%d" % b3, "rsA%d" % b2], w=["xn%d" % b2], out=xn[b2][:], in0=xt[b3][:],
                       scalar1=rs[b2][:, 0:1], scalar2=None, op0=ALU.mult)

            def a_s2(t):
                xb = t % 3
                b2 = t % 2
                j = 1 if t < 2 else 0
                for kc in range(8):
                    self.I("pe", "transpose", r=["xn%d" % xb, "identf"], w=["ptA%d" % b2], out=pt[b2][:, kc, :],
                           in_=xn[xb][:, kc * 128:(kc + 1) * 128], identity=self.identf[:])
                for kc in range(8):
                    o = self.hT[:, kc, t * 128:(t + 1) * 128]
                    a_ = self.A_fm[:, l, j, kc:kc + 1]
                    b_ = self.B_fm[:, l, j, kc:kc + 1]
                    if kc != 0:
                        self.I("dve", "tensor_scalar", r=["ptA%d" % b2, "A_fm", "B_fm"], w=["hT%d" % t], out=o,
                               in0=pt[b2][:, kc, :], scalar1=a_, scalar2=b_, op0=ALU.mult, op1=ALU.add)
                    else:
                        self.I("act", "activation", r=["ptA%d" % b2, "A_fm", "B_fm"], w=["hT%d" % t], out=o,
                               in_=pt[b2][:, kc, :], func=AF.Identity, scale=a_, bias=b_)

            a_s1(0)
            a_s1(1)

            def step(t):
                if t + 2 < NTL:
                    a_s1(t + 2)
                a_s2(t)
            return step

    def phase_B(self, l):
        nc, S = self.nc, self.S
        groups = []
        def tm(dst, c0, dc0, n, fn=None):
            groups.append(("TM", dst, c0, dc0, n, fn))
        def fm(dst, c0, dr0, n, fn=None, dt=BF16):
            groups.append(("FM", dst, c0, dr0, n, fn, dt))
        tm(self.GV, C_GV, 0, 512)
        fm(self.GQ, C_GQ, 0, 512, "qscale")
        fm(self.GK, C_GK, 0, 512)
        tm(self.GV, C_GV + 512, 512, 512)
        fm(self.GLR, C_LR, 0, 32, None, F32)
        fm(self.HQ, C_HQ, 0, 512); fm(self.HQ, C_HQ + 512, 512, 512)
        for i in range(4):
            fm(self.HZ, C_HF + i * 512, i * 512, 512, None, F32)
        tm(self.HV, C_HI, 0, 512); tm(self.HV, C_HI + 512, 512, 512)
        tm(self.AQ, C_AQ, 0, 512, "rope"); tm(self.AQ, C_AQ + 512, 512, 512, "rope")
        tm(self.AK, C_AK, 0, 256, "rope")
        tm(self.AV, C_AV, 0, 256)
        fm(self.GG, C_GG, 0, 512, "silu"); fm(self.GG, C_GG + 512, 512, 512, "silu")
        fm(self.HG, C_HG, 0, 512, "silu"); fm(self.HG, C_HG + 512, 512, 512, "silu")
        tm(self.AG, C_AG, 0, 512, "silu"); tm(self.AG, C_AG + 512, 512, 512, "silu")
        for i in range(6):
            fm(self.MG, C_MG + i * 512, i * 512, 512, "sigmoid")
        with ExitStack() as ph:
            a_step = self.phase_A(l, ph)
            T, P = self.TT(ph)
            wb = [T("wbB%d" % i, [128, 8, 512], BF16) for i in range(2)]
            stg = [T("stgB%d" % i, [128, 4, 512], BF16) for i in range(2)]
            stg32 = [T("stgB32_%d" % i, [128, 4, 512], F32) for i in range(2)]
            rope = T("ropeB", [128, 32, 128], F32)
            t1 = T("t1B", [128, 4, 2, 32], F32)
            t2 = T("t2B", [128, 4, 2, 32], F32)
            ps = [P("psB%d" % i, [128, 512], F32) for i in range(4)]
            S.dma("sp", rope[:], self.c_rope.rearrange("(t p) c -> p t c", p=128), writes=["ropeB"])
            pi = 0
            si = 0
            hk = ["hT%d" % t for t in range(NTL)]
            for gi, g in enumerate(groups):
                kind, dst, c0, d0, n = g[0], g[1], g[2], g[3], g[4]
                fn = g[5]
                w = wb[gi % 2]
                wk = "wbB%d" % (gi % 2)
                for half in range(2):
                    S.dma("pool", w[:, half * 4:(half + 1) * 4, 0:n],
                          self.w_in[l, half * 512:(half + 1) * 512, c0:c0 + n].rearrange("(k p) n -> p k n", p=128),
                          writes=[wk])
                if kind == "TM":
                    for t0 in range(0, NTL, 4):
                        nt = min(4, NTL - t0)
                        sb = stg[si % 2]
                        sk = "stgB%d" % (si % 2)
                        si += 1
                        for ti in range(nt):
                            t = t0 + ti
                            p = ps[pi % 4]
                            pk = "psB%d" % (pi % 4)
                            pi += 1
                            if gi == 0:
                                if t == 0:
                                    for t_ in range(min(ALAG, NTL)):
                                        a_step(t_)
                                if t + ALAG < NTL:
                                    a_step(t + ALAG)
                            for kc in range(8):
                                self.I("pe", "matmul", r=[hk[t], wk], w=[pk], out=p[:, 0:n],
                                       lhsT=self.hT[:, kc, t * 128:(t + 1) * 128], rhs=w[:, kc, 0:n],
                                       start=(kc == 0), stop=(kc == 7))
                            o = sb[:, ti, 0:n]
                            if fn == "silu":
                                self.I("act", "activation", r=[pk], w=[sk], out=o, in_=p[:, 0:n], func=AF.Silu)
                            elif fn == "rope" and t >= 2:
                                nh = n // 128
                                pv = p[:, 0:n].rearrange("p (h a b r) -> p h a b r", h=nh, a=2, b=2)
                                ov = o.rearrange("p (h a b r) -> p h a b r", h=nh, a=2, b=2)
                                rv = rope[:, t - 2, :].rearrange("p (s a r) -> p s a r", s=2, a=2)
                                cosb = rv[:, 0].unsqueeze(1).to_broadcast([128, nh, 2, 32])
                                sinb = rv[:, 1].unsqueeze(1).to_broadcast([128, nh, 2, 32])
                                u1 = pv[:, :, :, 0, :]
                                u2 = pv[:, :, :, 1, :]
                                a1 = t1[:, 0:nh]
                                a2 = t2[:, 0:nh]
                                self.I("dve", "tensor_tensor", r=[pk, "ropeB"], w=["t1B"], out=a1, in0=u1, in1=cosb, op=ALU.mult)
                                self.I("dve", "tensor_tensor", r=[pk, "ropeB"], w=["t2B"], out=a2, in0=u2, in1=sinb, op=ALU.mult)
                                self.I("dve", "tensor_tensor", r=["t1B", "t2B"], w=[sk], out=ov[:, :, :, 0, :], in0=a1, in1=a2, op=ALU.subtract)
                                self.I("dve", "tensor_tensor", r=[pk, "ropeB"], w=["t1B"], out=a1, in0=u2, in1=cosb, op=ALU.mult)
                                self.I("dve", "tensor_tensor", r=[pk, "ropeB"], w=["t2B"], out=a2, in0=u1, in1=sinb, op=ALU.mult)
                                self.I("dve", "tensor_tensor", r=["t1B", "t2B"], w=[sk], out=ov[:, :, :, 1, :], in0=a1, in1=a2, op=ALU.add)
                            else:
                                self.copy(self.alt(), o, p[:, 0:n], [pk], [sk])
                        S.dma("sp", dst[t0 * 128:(t0 + nt) * 128, d0:d0 + n].rearrange("(t p) n -> p t n", p=128),
                              sb[:, 0:nt, 0:n], reads=[sk])
                else:
                    dt = g[6]
                    nf = max(1, n // 128)
                    m = min(n, 128)
                    for tt in range(9):
                        wtok = 512 if tt < 8 else 256
                        sb = (stg if dt == BF16 else stg32)[si % 2]
                        sk = ("stgB%d" if dt == BF16 else "stgB32_%d") % (si % 2)
                        si += 1
                        for f in range(nf):
                            p = ps[pi % 4]
                            pk = "psB%d" % (pi % 4)
                            pi += 1
                            for kc in range(8):
                                self.I("pe", "matmul", r=hk[tt * 4:tt * 4 + 4] + [wk], w=[pk], out=p[0:m, 0:wtok],
                                       lhsT=w[:, kc, f * 128:f * 128 + m], rhs=self.hT[:, kc, tt * 512:tt * 512 + wtok],
                                       start=(kc == 0), stop=(kc == 7))
                            o = sb[0:m, f, 0:wtok]
                            if fn == "sigmoid":
                                self.I("act", "activation", r=[pk], w=[sk], out=o, in_=p[0:m, 0:wtok], func=AF.Sigmoid)
                            elif fn == "silu":
                                self.I("act", "activation", r=[pk], w=[sk], out=o, in_=p[0:m, 0:wtok], func=AF.Silu)
                            elif fn == "qscale":
                                if self.alt() == "act":
                                    self.I("act", "activation", r=[pk], w=[sk], out=o, in_=p[0:m, 0:wtok], func=AF.Copy, scale=128.0 ** -0.5)
                                else:
                                    self.I("dve", "tensor_scalar", r=[pk], w=[sk], out=o, in0=p[0:m, 0:wtok], scalar1=128.0 ** -0.5,
                                           scalar2=None, op0=ALU.mult)
                            else:
                                self.copy(self.alt(), o, p[0:m, 0:wtok], [pk], [sk])
                        if n >= 128:
                            S.dma("sp", dst[d0:d0 + n, tt * 512:tt * 512 + wtok].rearrange("(f p) t -> p f t", p=128),
                                  sb[:, 0:nf, 0:wtok], reads=[sk])
                        else:
                            S.dma("sp", dst[d0:d0 + n, tt * 512:tt * 512 + wtok], sb[0:m, 0, 0:wtok], reads=[sk])
            S.barrier()

    def phase_rec(self, l, mixer):
        nc, S = self.nc, self.S
        gla = mixer == "gla"
        bst = not gla
        H = 4 if gla else 8
        V = 256 if gla else 128
        Lc = 128 if gla else 32
        NCH = 128 // Lc
        HT = H * 128
        S2 = H * NCH
        scl = (-1.0 / 16.0) if gla else (-1.0 if l == 0 else 1.0)
        Qd = self.GQ if gla else self.HQ
        Vd = self.GV if gla else self.HV
        Gd = self.GG if gla else self.HG
        OF = self.OFG if gla else self.OFH
        Y = self.YG if gla else self.YH
        ngd = self.gla_ng if gla else self.hg_ng
        with ExitStack() as ph:
            T, P = self.TT(ph)
            qt = [T("qt%d" % i, [128, H, 128], BF16) for i in range(3)]
            vt = [T("vt%d" % i, [Lc, NCH, DM], BF16) for i in range(3)]
            gt = [T("gt%d" % i, [128, 8, 128], BF16) for i in range(3)]
            oft = [T("oft%d" % i, [128, 8, 128], F32) for i in range(3)]
            if gla:
                kt = [T("kt%d" % i, [128, H, 128], BF16) for i in range(3)]
                lrt = [T("lrt%d" % i, [33, 128], F32) for i in range(3)]
                wa2t = T("wa2t", [33, 512], F32)
                ps_z = P("ps_z", [128, H, 128], F32)
            else:
                zt = [T("zt%d" % i, [128, H, 128], F32) for i in range(3)]
                ktc = [T("ktc%d" % i, [128, H, 128], BF16) for i in range(4)]
                ff = T("ff", [128, H, 128], F32)
                lbr = T("lbr", [128, 2, 2, 8], F32)
                lbv = T("lbv", [128, 2, 8], F32)
                oml = T("oml", [128, 2, 8], F32)
            et = T("et", [128, HT], F32)
            xs = T("xs", [128, HT], F32)
            ones = T("ones", [128, HT], F32)
            Pext = T("Pext", [128, HT + 1], F32)
            Dm = T("Dm", [128, S2, Lc], F32)
            Eq = T("Eq", [128, HT], F32)
            Ek = T("Ek", [128, HT], F32)
            facin = T("facin", [128, 3, S2], F32)
            fac = [T("fac%d" % i, [128, 3, S2], F32) for i in range(4)]
            qtl = [T("qtl%d" % i, [128, H, 128], BF16) for i in range(3)]
            qsl = [T("qsl%d" % i, [128, H, 128], BF16) for i in range(3)]
            ktl = [T("ktl%d" % i, [128, H, 128], BF16) for i in range(3)]
            kul = [T("kul%d" % i, [128, H, 128], BF16) for i in range(3)]
            attm = [T("attm%d" % i, [Lc, H, Lc], BF16) for i in range(2)]
            kTs = [T("kTs%d" % i, [Lc, H, 128], BF16) for i in range(2)]
            Sb = [T("Sb%d" % i, [128, H, V], BF16) for i in range(2)]
            St = [T("St%d" % i, [128, H, V], F32) for i in range(2)] if not bst else None
            facb = [T("facb%d" % i, [128, S2], BF16) for i in range(4)] if bst else None
            t2b = T("t2b", [128, H, V], BF16) if bst else None
            t2 = T("t2", [128, H, V], F32) if not bst else None
            VB = V // 128
            osb = T("osb", [128, 8, 128], F32)
            sqb = T("sqb", [128, 8, 128], BF16)
            rst = T("rst", [128, H, 128], F32)
            onesb = T("onesb", [128, 128], BF16)
            yTs = [T("yTs%d" % i, [128, 8, 512], BF16) for i in range(2)]
            gain = T("gain", [128, 8], F32)
            ps_att = P("ps_att", [Lc, H, Lc], F32)
            ps_kT = P("ps_kT", [Lc, H, 128], BF16)
            ps_U = P("ps_U", [128, H, V], F32)
            ps_o = P("ps_o", [128, 8, 128], F32)
            ps_ss = P("ps_ss", [128, H, 128], F32)

            mfull = [T("mfull%d" % i, [Lc, H, Lc], mybir.dt.uint32) for i in range(2)]
            for i, (mm, mkk) in enumerate(((self.mle_u, "mle_u"), (self.mge_u, "mge_u"))):
                self.I("dve", "tensor_copy", r=[mkk], w=["mfull%d" % i], out=mfull[i][:],
                       in_=mm[0:Lc, 0:Lc].unsqueeze(1).to_broadcast([Lc, H, Lc]))
            self.I("pool", "memset", w=["ones"], ap=ones[:], constant=1.0)
            self.I("pool", "memset", w=["Pext"], ap=Pext[:, 0:1], constant=0.0)
            S.dma("sp", gain[:], ngd[:, l, :], writes=["gain"])
            self.I("pool", "memset", w=["onesb"], ap=onesb[:], constant=1.0)
            if gla:
                for i in range(3):
                    self.I("pool", "memset", w=["lrt%d" % i], ap=lrt[i][32:33, :], constant=1.0)
            elif l == 1:
                S.dma("sp", lbr[:], self.lb_fm, writes=["lbr"])
                self.I("dve", "tensor_tensor", r=["lbr"], w=["lbv"], out=lbv[:], in0=lbr[:, 1], in1=lbr[:, 0], op=ALU.subtract)
                self.I("act", "activation", r=["lbv"], w=["lbv"], out=lbv[:], in_=lbv[:], func=AF.Exp, scale=-1.0)
                self.I("dve", "tensor_scalar", r=["lbv"], w=["lbv"], out=lbv[:], in0=lbv[:], scalar1=1.0, scalar2=None, op0=ALU.add)
                self.I("dve", "reciprocal", r=["lbv"], w=["lbv"], out=lbv[:], in_=lbv[:])
                self.I("dve", "tensor_scalar", r=["lbv"], w=["oml"], out=oml[:], in0=lbv[:], scalar1=-1.0, scalar2=1.0,
                       op0=ALU.mult, op1=ALU.add)

            tile_it = [0]
            grp_cnt = {}
            for d in range(2):
                fwd = d == 0
                tiles = list(range(NTL)) if fwd else [1, 0] + list(range(NTL - 1, 1, -1))
                chunks = list(range(NCH)) if fwd else list(range(NCH - 1, -1, -1))
                mask = (self.mle_u if fwd else self.mge_u)
                mk = "mle_u" if fwd else "mge_u"
                sq_sign = scl if fwd else -scl
                i_ss, i_us = (0, 1) if fwd else (1, 0)
                if bst:
                    self.I("pool", "memset", w=["Sb1"], ap=Sb[1][:], constant=0.0)
                else:
                    self.I("pool", "memset", w=["St1"], ap=St[1][:], constant=0.0)
                for i in range(2):
                    self.I("pool", "memset", w=["attm%d" % i], ap=attm[i][:], constant=0.0)
                if gla:
                    S.dma("sp", wa2t[:], self.wa2p[l, d], writes=["wa2t"])
                tinfo = {}

                def prep_stages(t):
                    n = tile_it[0]
                    tile_it[0] += 1
                    tinfo[t] = n
                    tb = n % 3
                    zb = n % 2
                    fb = n % 4
                    tok = slice(t * 128, (t + 1) * 128)
                    need_y = (not fwd) and (t >= 2 or l == 0)
                    fk = "fac%d" % fb
                    ksrc = kt[tb] if gla else ktc[fb]
                    kkey = ("kt%d" % tb) if gla else ("ktc%d" % fb)
                    Pv0 = Pext[:, 0:HT].rearrange("p (s i) -> p s i", i=Lc)
                    Pv1 = Pext[:, 1:1 + HT].rearrange("p (s i) -> p s i", i=Lc)
                    ref = Pv0[:, :, Lc // 2:Lc // 2 + 1]
                    st0 = Pv0[:, :, 0:1]
                    en0 = Pv1[:, :, Lc - 1:Lc]
                    qf = qt[tb][:].rearrange("p h t -> p (h t)")

                    def sm1():
                        if gla:
                            S.dma("sp", lrt[zb][0:32, :], self.GLR[:, tok], writes=["lrt%d" % zb])
                        else:
                            S.dma("sp", zt[zb][:], self.HZ[d * 1024:(d + 1) * 1024, tok].rearrange("(h p) t -> p h t", p=128),
                                  writes=["zt%d" % zb])

                    def s0():
                        if gla:
                            for h in range(H):
                                self.I("pe", "matmul", r=["wa2t", "lrt%d" % zb], w=["ps_z"], out=ps_z[:, h, :],
                                       lhsT=wa2t[:, h * 128:(h + 1) * 128], rhs=lrt[zb][:, :], start=True, stop=True)
                            self.I("act", "activation", r=["ps_z"], w=["et"], out=et[:], in_=ps_z[:].rearrange("p h t -> p (h t)"),
                                   func=AF.Exp, scale=-1.0)
                        else:
                            zf = zt[zb][:].rearrange("p h t -> p (h t)")
                            self.I("act", "activation", r=["zt%d" % zb], w=["et"], out=et[:], in_=zf, func=AF.Exp, scale=-1.0)
                        self.I("act", "activation", r=["et", "one_t"], w=["xs"], out=xs[:], in_=et[:], func=AF.Ln,
                               bias=self.one_t[:, 0:1], scale=1.0)

                    def s1():
                        if not gla:
                            fv = ff[:].rearrange("p h t -> p (h t)")
                            self.I("act", "activation", r=["xs"], w=["ff"], out=fv, in_=xs[:], func=AF.Exp, scale=-1.0)
                            if l == 1:
                                self.I("dve", "tensor_tensor", r=["ff", "oml"], w=["ff"], out=ff[:], in0=ff[:],
                                       in1=oml[:, d, :].unsqueeze(2).to_broadcast([128, H, 128]), op=ALU.mult)
                                self.I("dve", "tensor_tensor", r=["ff", "lbv"], w=["ff"], out=ff[:], in0=ff[:],
                                       in1=lbv[:, d, :].unsqueeze(2).to_broadcast([128, H, 128]), op=ALU.add)
                                self.I("act", "activation", r=["ff"], w=["xs"], out=xs[:], in_=fv, func=AF.Ln)
                            self.I("act", "activation", r=["ff", "one_t"], w=[kkey], out=ktc[fb][:].rearrange("p h t -> p (h t)"), in_=fv,
                                   func=AF.Identity, scale=-1.0, bias=self.one_t[:, 0:1])
                        self.I("dve", "tensor_tensor_scan", r=["ones", "xs", "Pext"], w=["Pext"], out=Pext[:, 1:1 + HT], data0=ones[:],
                               data1=xs[:], initial=0.0, op0=ALU.mult, op1=ALU.add)

                    def s2():
                        self.I("dve", "tensor_tensor", r=["Pext"], w=["Dm"], out=Dm[:], in0=(Pv1 if fwd else Pv0),
                               in1=ref.to_broadcast([128, S2, Lc]), op=ALU.subtract)
                        self.I("dve", "tensor_tensor", r=["Pext"], w=["facin"], out=facin[:, 0, :].unsqueeze(2), in0=ref, in1=st0, op=ALU.subtract)
                        self.I("dve", "tensor_tensor", r=["Pext"], w=["facin"], out=facin[:, 2, :].unsqueeze(2), in0=en0, in1=st0, op=ALU.subtract)
                        self.I("dve", "tensor_tensor", r=["facin"], w=["facin"], out=facin[:, 1, :], in0=facin[:, 2, :], in1=facin[:, 0, :], op=ALU.subtract)

                    def s3():
                        S.dma("sp", qt[tb][:], Qd[:, tok].rearrange("(h p) t -> p h t", p=128), writes=["qt%d" % tb])
                        if gla:
                            S.dma("sp", kt[tb][:], self.GK[:, tok].rearrange("(h p) t -> p h t", p=128), writes=["kt%d" % tb])
                        self.I("act", "activation", r=["facin"], w=[fk], out=fac[fb][:], in_=facin[:], func=AF.Exp, scale=scl)
                        if bst:
                            self.I("act", "activation", r=["facin"], w=["facb%d" % fb], out=facb[fb][:], in_=facin[:, 2, :], func=AF.Exp, scale=scl)
                        Df = Dm[:].rearrange("p s i -> p (s i)")
                        self.I("act", "activation", r=["Dm"], w=["Eq"], out=Eq[:], in_=Df, func=AF.Exp, scale=sq_sign)
                        self.I("act", "activation", r=["Dm"], w=["Ek"], out=Ek[:], in_=Df, func=AF.Exp, scale=-sq_sign)

                    def s4l():
                        S.dma("sp", vt[tb][:], Vd[tok, :].rearrange("(c j) f -> j c f", j=Lc), writes=["vt%d" % tb])
                        if need_y:
                            S.dma("sp", oft[tb][:], OF[:, tok].rearrange("(f p) t -> p f t", p=128), reads=["OF%d" % t], writes=["oft%d" % tb])
                            S.dma("sp", gt[tb][:], Gd[:, tok].rearrange("(f p) t -> p f t", p=128), writes=["gt%d" % tb])

                    def s4a():
                        self.I("dve", "tensor_tensor", r=["qt%d" % tb, "Eq"], w=["qtl%d" % tb], out=qtl[tb][:].rearrange("p h t -> p (h t)"),
                               in0=qf, in1=Eq[:], op=ALU.mult)
                        self.I("dve", "tensor_tensor", r=[kkey, "Ek"], w=["ktl%d" % tb], out=ktl[tb][:].rearrange("p h t -> p (h t)"),
                               in0=ksrc[:].rearrange("p h t -> p (h t)"), in1=Ek[:], op=ALU.mult)


                    def s4b():
                        self.I("dve", "tensor_tensor", r=["ktl%d" % tb, fk], w=["kul%d" % tb],
                               out=kul[tb][:].rearrange("p h (c i) -> p (h c) i", i=Lc),
                               in0=ktl[tb][:].rearrange("p h (c i) -> p (h c) i", i=Lc),
                               in1=fac[fb][:, i_us, :].unsqueeze(2).to_broadcast([128, S2, Lc]), op=ALU.mult)

                    def s5():
                        self.I("dve", "tensor_tensor", r=["qtl%d" % tb, fk], w=["qsl%d" % tb],
                               out=qsl[tb][:].rearrange("p h (c i) -> p (h c) i", i=Lc),
                               in0=qtl[tb][:].rearrange("p h (c i) -> p (h c) i", i=Lc),
                               in1=fac[fb][:, i_ss, :].unsqueeze(2).to_broadcast([128, S2, Lc]), op=ALU.mult)

                    return [sm1, s0, s1, s2, s3, s4a, s4b, s5, s4l]

                seq = [(t, c) for t in tiles for c in chunks]
                G = len(seq)

                def fsel(fb_, r_, c):
                    return fac[fb_][:, r_, :].rearrange("p (h c) -> p h c", c=NCH)[:, :, c:c + 1].to_broadcast([128, H, V])

                def stA(g):
                    t, c = seq[g]
                    tb = tinfo[t] % 3
                    gb = g % 2
                    cs_ = slice(c * Lc, (c + 1) * Lc)
                    for h in range(H):
                        self.I("pe", "matmul", r=["ktl%d" % tb, "qtl%d" % tb], w=["ps_att"], out=ps_att[:, h, :],
                               lhsT=ktl[tb][:, h, cs_], rhs=qtl[tb][:, h, cs_], start=True, stop=True)
                    for h in range(H):
                        self.I("pe", "transpose", r=["kul%d" % tb, "identb"], w=["ps_kT"], out=ps_kT[:, h, :],
                               in_=kul[tb][:, h, cs_], identity=self.identb[:])
                    self.I("act", "activation", r=["ps_kT"], w=["kTs%d" % gb], out=kTs[gb][:], in_=ps_kT[:], func=AF.Copy)

                def stAm(g):
                    t, c = seq[g]
                    tb = tinfo[t] % 3
                    gb = g % 2
                    self.I("dve", "copy_predicated", r=["ps_att", "mfull%d" % d, "attm%d" % gb], w=["attm%d" % gb],
                           out=attm[gb][:].rearrange("p h i -> p (h i)"), mask=mfull[d][:].rearrange("p h i -> p (h i)"),
                           data=ps_att[:].rearrange("p h i -> p (h i)"))

                def stB(g):
                    t, c = seq[g]
                    tb = tinfo[t] % 3
                    fb_ = tinfo[t] % 4
                    gb = g % 2
                    for h in range(H):
                        vs = slice(h * V, (h + 1) * V)
                        self.I("pe", "matmul", r=["kTs%d" % gb, "vt%d" % tb], w=["ps_U"], out=ps_U[:, h, :],
                               lhsT=kTs[gb][:, h, :], rhs=vt[tb][:, c, vs], start=True, stop=True)

                def stSB(g):
                    if bst:
                        return
                    gb = g % 2
                    pb = (g - 1) % 2
                    self.I("act", "activation", r=["St%d" % pb], w=["Sb%d" % gb], out=Sb[gb][:], in_=St[pb][:], func=AF.Copy)

                def stC(g):
                    t, c = seq[g]
                    tb = tinfo[t] % 3
                    gb = g % 2
                    sbi = ((g - 1) % 2) if bst else gb
                    cs_ = slice(c * Lc, (c + 1) * Lc)
                    for h in range(H):
                        for vb in range(VB):
                            fb2 = h * VB + vb
                            self.I("pe", "matmul", r=["attm%d" % gb, "vt%d" % tb], w=["ps_o"], out=ps_o[:, fb2, cs_],
                                   lhsT=vt[tb][:, c, fb2 * 128:(fb2 + 1) * 128], rhs=attm[gb][:, h, :], start=True, stop=False)
                            self.I("pe", "matmul", r=["qsl%d" % tb, "Sb%d" % sbi], w=["ps_o"], out=ps_o[:, fb2, cs_],
                                   lhsT=Sb[sbi][:, h, vb * 128:(vb + 1) * 128], rhs=qsl[tb][:, h, cs_], start=False, stop=True)

                def stCH(g):
                    t, c = seq[g]
                    fb_ = tinfo[t] % 4
                    gb = g % 2
                    pb = (g - 1) % 2
                    if bst:
                        dsel = facb[fb_][:].rearrange("p (h c) -> p h c", c=NCH)[:, :, c:c + 1].to_broadcast([128, H, V])
                        self.I("dve", "tensor_tensor", r=["Sb%d" % pb, "facb%d" % fb_], w=["t2b"], out=t2b[:], in0=Sb[pb][:], in1=dsel, op=ALU.mult)
                        self.I("dve", "tensor_tensor", r=["t2b", "ps_U"], w=["Sb%d" % gb], out=Sb[gb][:], in0=ps_U[:], in1=t2b[:], op=ALU.add)
                        return
                    self.I("dve", "tensor_tensor", r=["St%d" % pb, "fac%d" % fb_], w=["t2"], out=t2[:], in0=St[pb][:], in1=fsel(fb_, 2, c), op=ALU.mult)
                    self.I("dve", "tensor_tensor", r=["t2", "ps_U"], w=["St%d" % gb], out=St[gb][:], in0=ps_U[:], in1=t2[:], op=ALU.add)

                def stE(t):
                    tb = tinfo[t] % 3
                    tok = slice(t * 128, (t + 1) * 128)
                    need_y = (not fwd) and (t >= 2 or l == 0)
                    if fwd:
                        self.I("act", "activation", r=["ps_o"], w=["osb"], out=osb[:], in_=ps_o[:], func=AF.Copy)
                        S.dma("pool", OF[:, tok].rearrange("(f p) t -> p f t", p=128), osb[:], reads=["osb"], writes=["OF%d" % t])
                        return
                    if not need_y:
                        return
                    self.I("dve", "tensor_tensor", r=["ps_o", "oft%d" % tb], w=["osb"], out=osb[:], in0=ps_o[:], in1=oft[tb][:], op=ALU.add)
                    self.I("act", "activation", r=["osb"], w=["sqb"], out=sqb[:], in_=osb[:], func=AF.Square)

                    def e_ss():
                        if VB == 1:
                            sf = sqb[:].rearrange("p f t -> p (f t)")
                            pf = ps_ss[:].rearrange("p f t -> p (f t)")
                            for hh in range(2):
                                self.I("pe", "matmul", r=["onesb", "sqb"], w=["ps_ss"], out=pf[:, hh * 512:(hh + 1) * 512],
                                       lhsT=onesb[:], rhs=sf[:, hh * 512:(hh + 1) * 512], start=True, stop=True)
                        else:
                            for h in range(H):
                                for vb in range(VB):
                                    self.I("pe", "matmul", r=["onesb", "sqb"], w=["ps_ss"], out=ps_ss[:, h, :],
                                           lhsT=onesb[:], rhs=sqb[:, h * VB + vb, :], start=(vb == 0), stop=(vb == VB - 1))
                        self.I("act", "activation", r=["ps_ss", "eps_t"], w=["rst"], out=rst[:], in_=ps_ss[:], func=AF.Ln, scale=1.0 / V,
                               bias=self.eps_t[:, 0:1])
                        self.I("act", "activation", r=["rst"], w=["rst"], out=rst[:], in_=rst[:], func=AF.Exp, scale=-0.5)

                    def e1b():
                        if VB == 1:
                            self.I("dve", "scalar_tensor_tensor", r=["osb", "rst", "gain"], w=["osb"], out=osb[:], in0=osb[:],
                                   scalar=gain[:, 0:1], in1=rst[:], op0=ALU.mult, op1=ALU.mult)
                        else:
                            ov = osb[:].rearrange("p (h v) t -> p h v t", v=VB)
                            for vb in range(VB):
                                self.I("dve", "scalar_tensor_tensor", r=["osb", "rst", "gain"], w=["osb"], out=ov[:, :, vb, :],
                                       in0=ov[:, :, vb, :], scalar=gain[:, vb:vb + 1], in1=rst[:], op0=ALU.mult, op1=ALU.mult)

                    def e1c():
                        if t >= 2:
                            g_ = (t - 2) // 4
                            pos = (t - 2) % 4
                            tok0 = 256 + g_ * 512
                            W = 512
                        else:
                            g_ = -1
                            pos = t
                            tok0 = 0
                            W = 256
                        sbuf_y = yTs[g_ % 2]
                        yk = "yTs%d" % (g_ % 2)
                        self.I("dve", "tensor_tensor", r=["osb", "gt%d" % tb], w=[yk], out=sbuf_y[:, :, pos * 128:(pos + 1) * 128],
                               in0=osb[:], in1=gt[tb][:], op=ALU.mult)
                        grp_cnt[g_] = grp_cnt.get(g_, 0) + 1
                        if grp_cnt[g_] == W // 128:
                            S.dma("pool", Y[:, tok0:tok0 + W].rearrange("(f p) t -> p f t", p=128), sbuf_y[:, :, 0:W], reads=[yk])
                            grp_cnt[g_] = 0

                    later.extend([e_ss, e1b, e1c])

                later = []
                NT_ = len(tiles)
                stage_lists = {}

                def emit_stage(k, n_):
                    if 0 <= n_ < NT_:
                        if n_ not in stage_lists:
                            stage_lists[n_] = prep_stages(tiles[n_])
                        stage_lists[n_][k]()

                def slot_items(m):
                    return [(7, m + 1), (5, m + 2), (6, m + 2), (4, m + 3), (3, m + 4), (2, m + 5), (1, m + 6), (0, m + 7), (8, m + 2)]

                def slot_split(m, j):
                    if NCH == 1:
                        return slot_items(m)
                    if j == 0:
                        return [(5, m + 2)]
                    if j == 1:
                        return [(6, m + 2), (4, m + 3), (3, m + 4)]
                    if j == 2:
                        return [(2, m + 5), (1, m + 6)]
                    return [(7, m + 1), (8, m + 2), (0, m + 7)]

                for n_ in range(NT_):
                    stage_lists[n_] = prep_stages(tiles[n_])
                for m in range(-7, 0):
                    for (k, n_) in slot_items(m):
                        emit_stage(k, n_)
                stA(0)
                stAm(0)
                stB(0)
                stSB(0)
                per_slot = -(-7 // NCH)
                for g in range(G):
                    t, c = seq[g]
                    m = g // NCH
                    j = g % NCH
                    last_of_tile = (j == NCH - 1)
                    if g + 1 < G:
                        stA(g + 1)
                    stC(g)
                    stCH(g)
                    if g + 1 < G:
                        stAm(g + 1)
                        stB(g + 1)
                        stSB(g + 1)
                    for _ in range(3 if NCH == 1 else 1):
                        if later:
                            later.pop(0)()
                    if last_of_tile:
                        stE(t)
                    for (k, n_) in slot_split(m, j):
                        emit_stage(k, n_)
                while later:
                    later.pop(0)()
                if d == 1:
                    S.barrier()

    def phase_att(self, l):
        nc, S = self.nc, self.S
        with ExitStack() as ph:
            T, P = self.TT(ph)
            kT_all = T("kT_all", [128, 2, NT], BF16)
            Vall = T("Vall", [128, NTL, 256], BF16)
            onesc = T("onesc", [128, 1], BF16)
            skt = T("skt", [128, 8], F32)
            esink = T("esink", [128, 8], F32)
            akt = [T("akt%d" % i, [128, 256], BF16) for i in range(2)]
            aqt = [T("aqt%d" % i, [128, DM], BF16) for i in range(2)]
            agt = [T("agt%d" % i, [128, DM], BF16) for i in range(2)]
            qT = [T("qT%d" % i, [128, 8, 128], BF16) for i in range(2)]
            ex = [[[T("ex%d_%d_%d" % (bb, k, i), [128, 512], BF16) for i in range(5)] for k in range(2)] for bb in range(2)]
            den = T("den", [128, 8], F32)
            attf = T("attf", [128, 8, 128], F32)
            yb = [T("yb%d" % i, [128, DM], BF16) for i in range(2)]
            yTs = [T("yTs%d" % i, [128, 8, 512], BF16) for i in range(2)]
            ps_kt = P("ps_kt", [128, 2, 128], BF16)
            ps_qT = P("ps_qT", [128, 8, 128], BF16)
            ps_s = [P("ps_s%d" % i, [128, 512], F32) for i in range(2)]
            ps_pv = P("ps_pv", [128, 8, 128], F32)
            ps_rs = P("ps_rs", [128, 8], F32)
            ps_yT = P("ps_yT", [128, 8, 128], BF16)

            self.I("pool", "memset", w=["onesc"], ap=onesc[:], constant=1.0)
            S.dma("sp", skt[:], self.sink[l, :].partition_broadcast(128), writes=["skt"])
            self.I("act", "activation", r=["skt"], w=["esink"], out=esink[:], in_=skt[:], func=AF.Exp)
            S.dma("sp", Vall[:], self.AV.rearrange("(t p) f -> p t f", p=128), writes=["Vall"])
            for t in range(NTL):
                b = t % 2
                S.dma("sp", akt[b][:], self.AK[t * 128:(t + 1) * 128, :], writes=["akt%d" % b])
                for kv in range(2):
                    self.I("pe", "transpose", r=["akt%d" % b, "identb"], w=["ps_kt"], out=ps_kt[:, kv, :],
                           in_=akt[b][:, kv * 128:(kv + 1) * 128], identity=self.identb[:])
                self.copy(self.alt(), kT_all[:, :, t * 128:(t + 1) * 128], ps_kt[:], ["ps_kt"], ["kT_all"])
            blocks = []
            if l == 0:
                blocks += [(0, [(0, None), (1, None)]), (1, [(0, None), (1, None)])]
            for n in range(32):
                keys = [(0, None), (1, None)]
                if n >= 1:
                    keys.append((n + 1, "ge"))
                keys.append((n + 2, None))
                if n <= 30:
                    keys.append((n + 3, "le"))
                blocks.append((n + 2, keys))
            sc_cnt = [0]
            grp_cnt = {}

            def stF(bi):
                t, keys = blocks[bi]
                b = bi % 2
                tok = slice(t * 128, (t + 1) * 128)
                S.dma("sp", aqt[b][:], self.AQ[tok, :], writes=["aqt%d" % b])
                S.dma("sp", agt[b][:], self.AG[tok, :], writes=["agt%d" % b])
                for h in range(8):
                    self.I("pe", "transpose", r=["aqt%d" % b, "identb"], w=["ps_qT"], out=ps_qT[:, h, :],
                           in_=aqt[b][:, h * 128:(h + 1) * 128], identity=self.identb[:])
                self.I("dve", "tensor_copy", r=["ps_qT"], w=["qT%d" % b], out=qT[b][:], in_=ps_qT[:])
                for kv in range(2):
                    for ki, (ktile, mt) in enumerate(keys):
                        p = ps_s[sc_cnt[0] % 2]
                        pk = "ps_s%d" % (sc_cnt[0] % 2)
                        sc_cnt[0] += 1
                        e = ex[b][kv][ki]
                        ek = "ex%d_%d_%d" % (b, kv, ki)
                        self.I("pe", "matmul", r=["kT_all", "qT%d" % b], w=[pk], out=p[:],
                               lhsT=kT_all[:, kv, ktile * 128:(ktile + 1) * 128],
                               rhs=qT[b][:, kv * 4:(kv + 1) * 4, :].rearrange("p h t -> p (h t)"), start=True, stop=True)
                        self.I("act", "activation", r=[pk], w=[ek], out=e[:], in_=p[:], func=AF.Exp, scale=128.0 ** -0.5)
                        if mt is not None:
                            mb = self.mgeb if mt == "ge" else self.mleb
                            ev = e[:].rearrange("p (g i) -> p g i", g=4)
                            self.I("dve", "tensor_tensor", r=[ek, "mgeb", "mleb"], w=[ek], out=ev, in0=ev,
                                   in1=mb[:].unsqueeze(1).to_broadcast([128, 4, 128]), op=ALU.mult)

            def stG1(bi):
                t, keys = blocks[bi]
                b = bi % 2
                nk = len(keys)
                for kv in range(2):
                    for g in range(4):
                        for ki, (ktile, mt) in enumerate(keys):
                            self.I("pe", "matmul", r=["ex%d_%d_%d" % (b, kv, ki), "Vall"], w=["ps_pv"], out=ps_pv[:, kv * 4 + g, :],
                                   lhsT=ex[b][kv][ki][:, g * 128:(g + 1) * 128], rhs=Vall[:, ktile, kv * 128:(kv + 1) * 128],
                                   start=(ki == 0), stop=(ki == nk - 1))
                    for g in range(4):
                        for ki, (ktile, mt) in enumerate(keys):
                            self.I("pe", "matmul", r=["ex%d_%d_%d" % (b, kv, ki), "onesc"], w=["ps_rs"],
                                   out=ps_rs[:, kv * 4 + g:kv * 4 + g + 1],
                                   lhsT=ex[b][kv][ki][:, g * 128:(g + 1) * 128], rhs=onesc[:, 0:1],
                                   start=(ki == 0), stop=(ki == nk - 1))
                self.I("dve", "tensor_tensor", r=["ps_rs", "esink"], w=["den"], out=den[:], in0=ps_rs[:], in1=esink[:], op=ALU.add)
                self.I("dve", "reciprocal", r=["den"], w=["den"], out=den[:], in_=den[:])
                self.I("dve", "tensor_tensor", r=["ps_pv", "den"], w=["attf"], out=attf[:], in0=ps_pv[:],
                       in1=den[:].unsqueeze(2).to_broadcast([128, 8, 128]), op=ALU.mult)
                self.I("dve", "tensor_tensor", r=["attf", "agt%d" % b], w=["yb%d" % b], out=yb[b][:],
                       in0=attf[:].rearrange("p h d -> p (h d)"), in1=agt[b][:], op=ALU.mult)

            def stG2(bi):
                t, keys = blocks[bi]
                b = bi % 2
                for fc in range(8):
                    self.I("pe", "transpose", r=["yb%d" % b, "identb"], w=["ps_yT"], out=ps_yT[:, fc, :],
                           in_=yb[b][:, fc * 128:(fc + 1) * 128], identity=self.identb[:])
                if t >= 2:
                    g_ = (t - 2) // 4
                    pos = (t - 2) % 4
                    tok0 = 256 + g_ * 512
                    W = 512
                else:
                    g_ = -1
                    pos = t
                    tok0 = 0
                    W = 256
                sb = yTs[g_ % 2]
                yk = "yTs%d" % (g_ % 2)
                self.I("act", "activation", r=["ps_yT"], w=[yk], out=sb[:, :, pos * 128:(pos + 1) * 128], in_=ps_yT[:], func=AF.Copy)
                grp_cnt[g_] = grp_cnt.get(g_, 0) + 1
                if grp_cnt[g_] == W // 128:
                    S.dma("pool", self.YA[:, tok0:tok0 + W].rearrange("(f p) t -> p f t", p=128), sb[:, :, 0:W], reads=[yk])

            nb = len(blocks)
            stF(0)
            for bi in range(nb):
                if bi + 1 < nb:
                    stF(bi + 1)
                stG1(bi)
                if bi >= 1:
                    stG2(bi - 1)
            stG2(nb - 1)
            S.barrier()

    def phase_merge(self, l, last):
        nc, S = self.nc, self.S
        with ExitStack() as ph:
            T, P = self.TT(ph)
            wbr, wo = self.wbr, self.wo
            yT = [[T("yTm%d_%d" % (i, n), [128, 8, 512], BF16) for n in range(3)] for i in range(2)]
            mgt = T("mgt", [128, 24, 512], BF16)
            merged = T("merged", [128, 8, 512], BF16)
            mt0 = T("mt0", [128, 512], F32)
            mt1 = T("mt1", [128, 512], F32)
            xt = [T("xtm%d" % i, [128, DM], F32) for i in range(2)]
            xo = [T("xom%d" % i, [128, DM], F32) for i in range(2)]
            fg = T("fg", [128, DM], F32)
            junk = T("junkm", [128, DM], BF16)
            ss = T("ssm", [128, 1], F32)
            rs = T("rsm", [128, 1], F32)
            ps_p = [P("ps_p%d" % i, [128, 512], F32) for i in range(6)]
            ps_out = P("ps_out", [128, DM], F32)
            if last:
                S.dma("sp", fg[:], self.final_g.partition_broadcast(128), writes=["fg"])
            tts = []
            if l == 0:
                tts.append((0, 256, 1))
            for i in range(8):
                tts.append((256 + i * 512, 512, 0))
            Ys = [self.YG, self.YH, self.YA]
            pc = 0
            xc = 0
            for ti, (tok0, W, j) in enumerate(tts):
                b = ti % 2
                for n in range(3):
                    S.dma("sp", yT[b][n][:, :, 0:W], Ys[n][:, tok0:tok0 + W].rearrange("(k p) t -> p k t", p=128),
                          writes=["yTm%d_%d" % (b, n)])
                mgsrc = self.MG[:, tok0:tok0 + W].rearrange("(n d p) t -> p n d t", n=3, d=8)
                mgdst = mgt[:].rearrange("p (n d) t -> p n d t", n=3)
                for dc in range(8):
                    S.dma("sp", mgdst[:, :, dc, 0:W], mgsrc[:, :, dc, :], writes=["mgt%d" % dc])
                for dc in range(8):
                    pp = []
                    for n in range(3):
                        p = ps_p[pc % 6]
                        pk = "ps_p%d" % (pc % 6)
                        pc += 1
                        pp.append((p, pk))
                        for kc in range(8):
                            self.I("pe", "matmul", r=["wbr", "yTm%d_%d" % (b, n)], w=[pk], out=p[:, 0:W],
                                   lhsT=wbr[:, n, kc, dc * 128:(dc + 1) * 128], rhs=yT[b][n][:, kc, 0:W],
                                   start=(kc == 0), stop=(kc == 7))
                    self.I("dve", "tensor_tensor", r=[pp[0][1], "mgt%d" % dc], w=["mt0"], out=mt0[:, 0:W], in0=pp[0][0][:, 0:W],
                           in1=mgt[:, dc, 0:W], op=ALU.mult)
                    self.I("dve", "tensor_tensor", r=[pp[1][1], "mgt%d" % dc], w=["mt1"], out=mt1[:, 0:W], in0=pp[1][0][:, 0:W],
                           in1=mgt[:, 8 + dc, 0:W], op=ALU.mult)
                    self.I("dve", "tensor_tensor", r=["mt0", "mt1"], w=["mt0"], out=mt0[:, 0:W], in0=mt0[:, 0:W], in1=mt1[:, 0:W], op=ALU.add)
                    self.I("dve", "tensor_tensor", r=[pp[2][1], "mgt%d" % dc], w=["mt1"], out=mt1[:, 0:W], in0=pp[2][0][:, 0:W],
                           in1=mgt[:, 16 + dc, 0:W], op=ALU.mult)
                    self.I("dve", "tensor_tensor", r=["mt0", "mt1"], w=["merged"], out=merged[:, dc, 0:W], in0=mt0[:, 0:W], in1=mt1[:, 0:W], op=ALU.add)
                for s_ in range(W // 128):
                    xb = xc % 2
                    xc += 1
                    r0 = tok0 + s_ * 128
                    if j == 1:
                        src = self.ctx_in[r0:r0 + 128, :]
                        dst = self.cs[r0:r0 + 128, :]
                    else:
                        xr = r0 - 256
                        src = (self.x_in if l == 0 else self.xs)[xr:xr + 128, :]
                        dst = (self.out if last else self.xs)[xr:xr + 128, :]
                    S.dma("sp", xt[xb][:], src, writes=["xtm%d" % xb])
                    for n2 in range(2):
                        for kc in range(8):
                            self.I("pe", "matmul", r=["merged", "wo"], w=["ps_out"], out=ps_out[:, n2 * 512:(n2 + 1) * 512],
                                   lhsT=merged[:, kc, s_ * 128:(s_ + 1) * 128], rhs=wo[:, kc, n2 * 512:(n2 + 1) * 512],
                                   start=(kc == 0), stop=(kc == 7))
                    gk = "gate_bc%d%d" % (l, j)
                    self.I("dve", "tensor_tensor", r=["ps_out", gk], w=["xom%d" % xb], out=xo[xb][:], in0=ps_out[:],
                           in1=self.gate_bc[l][j][:], op=ALU.mult)
                    self.I("dve", "tensor_tensor", r=["xom%d" % xb, "xtm%d" % xb], w=["xom%d" % xb], out=xo[xb][:], in0=xo[xb][:],
                           in1=xt[xb][:], op=ALU.add)
                    if last:
                        self.I("act", "activation", r=["xom%d" % xb], w=["junkm", "ssm"], out=junk[:], in_=xo[xb][:],
                               func=AF.Square, accum_out=ss[:])
                        self.I("act", "activation", r=["ssm", "eps_t"], w=["rsm"], out=rs[:], in_=ss[:], func=AF.Ln,
                               scale=1.0 / DM, bias=self.eps_t[:, 0:1])
                        self.I("act", "activation", r=["rsm"], w=["rsm"], out=rs[:], in_=rs[:], func=AF.Exp, scale=-0.5)
                        self.I("dve", "scalar_tensor_tensor", r=["xom%d" % xb, "rsm", "fg"], w=["xom%d" % xb], out=xo[xb][:],
                               in0=xo[xb][:], scalar=rs[:, 0:1], in1=fg[:], op0=ALU.mult, op1=ALU.mult)
                    S.dma("pool", dst, xo[xb][:], reads=["xom%d" % xb])
            S.barrier()


def host_inputs(x, c, ctx, c_ctx, norm_g, w_ada, b_ada, w_in, gla_w_a2, gla_b_a2, gla_norm_g,
                hgrn_lb_logits, hgrn_norm_g, attn_sink, w_branch, w_out, final_g):
    f = lambda a: np.ascontiguousarray(np.asarray(a, dtype=np.float32))
    x, c, ctx, c_ctx = f(x), f(c), f(ctx), f(c_ctx)
    shared = {}
    shared["ng_fm"] = f(f(norm_g).reshape(2, 8, 128).transpose(2, 0, 1))
    shared["w_ada"] = f(w_ada)
    shared["bada_fm"] = f(f(b_ada).reshape(2, 24, 128).transpose(2, 0, 1))
    shared["b_ada"] = f(b_ada)
    shared["w_in"] = f(w_in)
    wa2p = np.zeros((2, 2, 33, 512), np.float32)
    wa2 = f(gla_w_a2)
    wa2p[:, 0, 0:16, :] = wa2[:, 0]
    wa2p[:, 1, 16:32, :] = wa2[:, 1]
    wa2p[:, :, 32, :] = f(gla_b_a2)
    shared["wa2p"] = wa2p
    gng = f(gla_norm_g).reshape(2, 2, 128)
    shared["gla_ng_fm"] = f(np.tile(gng.transpose(2, 0, 1)[:, :, None, :], (1, 1, 4, 1)).reshape(128, 2, 8))
    hng = f(hgrn_norm_g)
    shared["hg_ng_fm"] = f(np.tile(hng.T[:, :, None], (1, 1, 8)))
    shared["sink"] = f(attn_sink)
    shared["final_g"] = f(final_g)
    shared["lb_fm"] = f(f(hgrn_lb_logits).reshape(2, 2, 8, 128).transpose(3, 0, 1, 2))
    shared["w_branch"] = f(w_branch)
    shared["w_out"] = f(w_out)
    shared["c_ident"] = np.eye(128, dtype=np.float32)
    jj = np.arange(128)[:, None]
    ii = np.arange(128)[None, :]
    shared["c_mle"] = (jj <= ii).astype(np.float32)
    shared["c_mge"] = (jj >= ii).astype(np.float32)
    shared["c_mle_u"] = (jj <= ii).astype(np.uint32)
    shared["c_mge_u"] = (jj >= ii).astype(np.uint32)
    t = np.arange(4096)
    inv = (10000.0 ** (-np.arange(32, dtype=np.float32) / 32)).astype(np.float32)
    ar = (t // 64).astype(np.float32)[:, None] * inv
    ac = (t % 64).astype(np.float32)[:, None] * inv
    rope = np.concatenate([np.cos(ar), np.cos(ac), np.sin(ar), np.sin(ac)], axis=1).astype(np.float32)
    shared["c_rope"] = f(rope)
    maps = []
    for b in range(x.shape[0]):
        m = dict(shared)
        m["x"] = f(x[b])
        m["ctx"] = f(ctx[b])
        cc = np.stack([c[b], c_ctx], axis=-1).reshape(8, 128, 2).transpose(1, 0, 2)
        m["cc"] = f(cc)
        maps.append(m)
    return maps


_NC_CACHE = {}


def kernel(**inputs):
    maps = host_inputs(**inputs)
    if "nc" not in _NC_CACHE:
        _NC_CACHE["nc"] = Builder().build()
    nc = _NC_CACHE["nc"]
    res = run_bass_kernel_spmd(nc, maps, core_ids=list(range(8)))
    return np.stack([np.asarray(r["out"], dtype=np.float32) for r in res.results], axis=0)
```
